# Optimizing a Trainium2 kernel written in Bass

```python
import math
import jax
import jax.numpy as jnp
from jax import lax
import numpy as np

D_MODEL = 1024
BATCH = 2
SEQ = 8192
DEPTH = 2

HEAD_DIM = 64
ROPE_THETA = 10000.0
NORM_EPS = 1e-6
PLE_DIM = 256
BLOCK = 128

A_HEADS = 8
IDX_HEADS = 8
IDX_DIM = 64
TOPK_MAX = 256

B_HEADS = 8
B_KV_HEADS = 2
B_WINDOW = 128

C_HEADS = 8
C_PATTERNS = ((128, 1), (512, 4), (2048, 16))

D_HEADS = 4
D_QK_DIM = 64
D_V_DIM = 128

A_WIDTH = A_HEADS * HEAD_DIM
B_WIDTH = B_HEADS * HEAD_DIM
C_WIDTH = C_HEADS * HEAD_DIM
D_WIDTH = D_HEADS * D_V_DIM

EVEN_SPLITS = (A_WIDTH, HEAD_DIM, HEAD_DIM, IDX_HEADS * IDX_DIM, IDX_DIM, IDX_HEADS, A_WIDTH,
               B_WIDTH, B_KV_HEADS * HEAD_DIM, B_KV_HEADS * HEAD_DIM, B_WIDTH)
ODD_SPLITS = (C_WIDTH, C_WIDTH, C_WIDTH, C_WIDTH,
              D_HEADS * 2 * D_QK_DIM, D_HEADS * 2 * D_QK_DIM, D_WIDTH, D_WIDTH)

kernel_name = 'hybrid_dsa_swa_dilated_diff_trunk'


def _rms_norm(x, gain):
    xf = x.astype(jnp.float32)
    y = xf * lax.rsqrt(jnp.mean(xf * xf, axis=-1, keepdims=True) + NORM_EPS)
    return (y * gain.astype(jnp.float32)).astype(x.dtype)


def _rope(x, pos):
    half = x.shape[-1] // 2
    inv = ROPE_THETA ** (-jnp.arange(half, dtype=jnp.float32) / half)
    ang = pos.astype(jnp.float32)[:, None] * inv[None, :]
    cos = jnp.cos(ang)[None, :, None, :]
    sin = jnp.sin(ang)[None, :, None, :]
    xf = x.astype(jnp.float32)
    x1, x2 = xf[..., :half], xf[..., half:]
    return jnp.concatenate([x1 * cos - x2 * sin, x2 * cos + x1 * sin], axis=-1).astype(x.dtype)


def _split(h, sizes):
    offs = np.cumsum(sizes)[:-1].tolist()
    return jnp.split(h, offs, axis=-1)


def _to_blocks(x):
    b, s = x.shape[:2]
    x = x.reshape((b, s // BLOCK, BLOCK) + x.shape[2:])
    return jnp.moveaxis(x, 1, 0)


def _from_blocks(x):
    x = jnp.moveaxis(x, 0, 1)
    return x.reshape((x.shape[0], x.shape[1] * x.shape[2]) + x.shape[3:])


def _dsa_attention(q, k, v, q_idx, k_idx, w_idx):
    bsz, seq = q.shape[:2]
    top_k = min(TOPK_MAX, seq // 4)
    key_pos = jnp.arange(seq)
    bidx = jnp.arange(bsz)[:, None, None]
    scale = HEAD_DIM ** -0.5
    idx_scale = (IDX_HEADS * IDX_DIM) ** -0.5

    def block(args):
        qb, qib, wib, start = args
        t = start + jnp.arange(BLOCK)
        causal = key_pos[None, :] <= t[:, None]
        rel = jax.nn.relu(jnp.einsum('bqhd,bsd->bqhs', qib, k_idx,
                                     preferred_element_type=jnp.float32))
        score = jnp.einsum('bqhs,bqh->bqs', rel, wib.astype(jnp.float32)) * idx_scale
        score = jnp.where(causal[None], score, -jnp.inf)
        _, sel = lax.top_k(score, top_k)
        valid = sel <= t[None, :, None]
        kg = k[bidx, sel]
        vg = v[bidx, sel]
        s = jnp.einsum('bqhd,bqkd->bqhk', qb, kg, preferred_element_type=jnp.float32) * scale
        s = jnp.where(valid[:, :, None, :], s, -jnp.inf)
        pr = jax.nn.softmax(s, axis=-1).astype(v.dtype)
        return jnp.einsum('bqhk,bqkd->bqhd', pr, vg)

    starts = jnp.arange(seq // BLOCK) * BLOCK
    out = lax.map(block, (_to_blocks(q), _to_blocks(q_idx), _to_blocks(w_idx), starts))
    return _from_blocks(out)


def _sliding_window_sink_attention(q, k, v, sinks):
    bsz, seq = q.shape[:2]
    nb = seq // BLOCK
    grp = B_HEADS // B_KV_HEADS
    scale = HEAD_DIM ** -0.5
    qb = q.reshape(bsz, nb, BLOCK, B_KV_HEADS, grp, HEAD_DIM)

    def band(x):
        xb = x.reshape(bsz, nb, BLOCK, B_KV_HEADS, HEAD_DIM)
        prev = jnp.pad(xb, ((0, 0), (1, 0), (0, 0), (0, 0), (0, 0)))[:, :-1]
        return jnp.concatenate([prev, xb], axis=2)

    kb, vb = band(k), band(v)
    s = jnp.einsum('bnqkgd,bnskd->bnkgqs', qb, kb, preferred_element_type=jnp.float32) * scale
    qi = jnp.arange(BLOCK)[:, None]
    kj = jnp.arange(2 * BLOCK)[None, :]
    dist = qi + BLOCK - kj
    in_win = (dist >= 0) & (dist < B_WINDOW)
    mask = in_win[None] & ((jnp.arange(nb)[:, None, None] > 0) | (kj[None] >= BLOCK))
    s = jnp.where(mask[None, :, None, None], s, -jnp.inf)
    sink = sinks.astype(jnp.float32).reshape(1, 1, B_KV_HEADS, grp, 1, 1)
    m = jnp.maximum(jnp.max(s, axis=-1, keepdims=True), sink)
    e = jnp.exp(s - m)
    denom = jnp.sum(e, axis=-1, keepdims=True) + jnp.exp(sink - m)
    pr = (e / denom).astype(v.dtype)
    o = jnp.einsum('bnkgqs,bnskd->bnqkgd', pr, vb)
    return o.reshape(bsz, seq, B_HEADS, HEAD_DIM)


def _dilated_attention(q, k, v):
    seq = q.shape[1]
    scale = HEAD_DIM ** -0.5

    def block(args):
        qb, start = args
        t = start + jnp.arange(BLOCK)
        lses, outs = [], []
        for window, dilation in C_PATTERNS:
            n_keys = window // dilation + 1
            idx = t[:, None] - dilation * jnp.arange(n_keys)[None, :]
            valid = idx >= 0
            idx = jnp.maximum(idx, 0)
            kg = k[:, idx]
            vg = v[:, idx]
            s = jnp.einsum('bqhd,bqnhd->bqhn', qb, kg, preferred_element_type=jnp.float32) * scale
            s = jnp.where(valid[None, :, None, :], s, -jnp.inf)
            m = jnp.max(s, axis=-1, keepdims=True)
            e = jnp.exp(s - m)
            den = jnp.sum(e, axis=-1, keepdims=True)
            o = jnp.einsum('bqhn,bqnhd->bqhd', e.astype(v.dtype), vg,
                           preferred_element_type=jnp.float32) / den
            lses.append(m[..., 0] + jnp.log(den[..., 0]))
            outs.append(o)
        wts = jax.nn.softmax(jnp.stack(lses, axis=0), axis=0)
        comb = jnp.sum(wts[..., None] * jnp.stack(outs, axis=0), axis=0)
        return comb.astype(q.dtype)

    starts = jnp.arange(seq // BLOCK) * BLOCK
    return _from_blocks(lax.map(block, (_to_blocks(q), starts)))


def _diff_attention(q, k, v, lam, sub_gain, lambda_init):
    seq = q.shape[1]
    key_pos = jnp.arange(seq)
    scale = D_QK_DIM ** -0.5

    def block(args):
        qb, start = args
        t = start + jnp.arange(BLOCK)
        s = jnp.einsum('bqhcd,bshcd->bhcqs', qb, k, preferred_element_type=jnp.float32) * scale
        causal = key_pos[None, :] <= t[:, None]
        s = jnp.where(causal, s, -jnp.inf)
        pr = jax.nn.softmax(s, axis=-1)
        a = pr[:, :, 0] - lam * pr[:, :, 1]
        return jnp.einsum('bhqs,bshd->bqhd', a.astype(v.dtype), v)

    starts = jnp.arange(seq // BLOCK) * BLOCK
    o = _from_blocks(lax.map(block, (_to_blocks(q), starts)))
    return _rms_norm(o, sub_gain) * (1.0 - lambda_init)


def _even_layer(h, pos, w_in, w_out, a_q_gain, a_k_gain, idx_k_gain, b_q_gain, b_k_gain, b_sinks):
    bsz, seq, _ = h.shape
    aq, ak, av, iq, ik, iw, ag, bq, bk, bv, bg = _split(h @ w_in, EVEN_SPLITS)
    aq = _rope(_rms_norm(aq.reshape(bsz, seq, A_HEADS, HEAD_DIM), a_q_gain), pos)
    ak = _rope(_rms_norm(ak.reshape(bsz, seq, 1, HEAD_DIM), a_k_gain), pos)[:, :, 0]
    iq = _rope(iq.reshape(bsz, seq, IDX_HEADS, IDX_DIM), pos)
    ik = _rope(_rms_norm(ik.reshape(bsz, seq, 1, IDX_DIM), idx_k_gain), pos)[:, :, 0]
    oa = _dsa_attention(aq, ak, av, iq, ik, iw)
    bq = _rope(_rms_norm(bq.reshape(bsz, seq, B_HEADS, HEAD_DIM), b_q_gain), pos)
    bk = _rope(_rms_norm(bk.reshape(bsz, seq, B_KV_HEADS, HEAD_DIM), b_k_gain), pos)
    bv = bv.reshape(bsz, seq, B_KV_HEADS, HEAD_DIM)
    ob = _sliding_window_sink_attention(bq, bk, bv, b_sinks)
    ya = oa.reshape(bsz, seq, A_WIDTH) * jax.nn.silu(ag)
    yb = ob.reshape(bsz, seq, B_WIDTH) * jax.nn.silu(bg)
    return jnp.concatenate([ya, yb], axis=-1) @ w_out


def _odd_layer(h, pos, w_in, w_out, c_q_gain, c_k_gain, d_q_gain, d_k_gain,
               lq1, lk1, lq2, lk2, sub_gain, lambda_init):
    bsz, seq, _ = h.shape
    cq, ck, cv, cg, dq, dk, dv, dg = _split(h @ w_in, ODD_SPLITS)
    cq = _rope(_rms_norm(cq.reshape(bsz, seq, C_HEADS, HEAD_DIM), c_q_gain), pos)
    ck = _rope(_rms_norm(ck.reshape(bsz, seq, C_HEADS, HEAD_DIM), c_k_gain), pos)
    cv = cv.reshape(bsz, seq, C_HEADS, HEAD_DIM)
    oc = _dilated_attention(cq, ck, cv)
    dq = _rms_norm(dq.reshape(bsz, seq, 2 * D_HEADS, D_QK_DIM), d_q_gain)
    dk = _rms_norm(dk.reshape(bsz, seq, 2 * D_HEADS, D_QK_DIM), d_k_gain)
    dq = _rope(dq, pos).reshape(bsz, seq, D_HEADS, 2, D_QK_DIM)
    dk = _rope(dk, pos).reshape(bsz, seq, D_HEADS, 2, D_QK_DIM)
    dv = dv.reshape(bsz, seq, D_HEADS, D_V_DIM)
    f32 = jnp.float32
    lam = (jnp.exp(jnp.sum(lq1.astype(f32) * lk1.astype(f32)))
           - jnp.exp(jnp.sum(lq2.astype(f32) * lk2.astype(f32))) + lambda_init)
    od = _diff_attention(dq, dk, dv, lam, sub_gain, lambda_init)
    yc = oc.reshape(bsz, seq, C_WIDTH) * jax.nn.silu(cg)
    yd = od.reshape(bsz, seq, D_WIDTH) * jax.nn.silu(dg)
    return jnp.concatenate([yc, yd], axis=-1) @ w_out


def setup_inputs(seed: int = 0) -> dict:
    key = jax.random.key(seed)
    ks = jax.random.split(key, 25)
    n_even = (DEPTH + 1) // 2
    n_odd = DEPTH // 2

    def nrm(k, shape, scale):
        return jax.random.normal(k, shape, jnp.float32) * scale

    def gain(k, shape):
        return 1.0 + 0.02 * jax.random.normal(k, shape, jnp.float32)

    even_out = A_WIDTH + B_WIDTH
    odd_out = C_WIDTH + D_WIDTH
    return {
        'x': nrm(ks[0], (BATCH, SEQ, D_MODEL), 1.0),
        'p': nrm(ks[1], (DEPTH, BATCH, SEQ, PLE_DIM), 1.0),
        'norm_gain': gain(ks[2], (DEPTH, D_MODEL)),
        'w_in_even': nrm(ks[3], (n_even, D_MODEL, sum(EVEN_SPLITS)), D_MODEL ** -0.5),
        'w_out_even': nrm(ks[4], (n_even, even_out, D_MODEL), even_out ** -0.5),
        'a_q_gain': gain(ks[5], (n_even, HEAD_DIM)),
        'a_k_gain': gain(ks[6], (n_even, HEAD_DIM)),
        'idx_k_gain': gain(ks[7], (n_even, IDX_DIM)),
        'b_q_gain': gain(ks[8], (n_even, HEAD_DIM)),
        'b_k_gain': gain(ks[9], (n_even, HEAD_DIM)),
        'b_sinks': nrm(ks[10], (n_even, B_HEADS), 0.5),
        'w_in_odd': nrm(ks[11], (n_odd, D_MODEL, sum(ODD_SPLITS)), D_MODEL ** -0.5),
        'w_out_odd': nrm(ks[12], (n_odd, odd_out, D_MODEL), odd_out ** -0.5),
        'c_q_gain': gain(ks[13], (n_odd, HEAD_DIM)),
        'c_k_gain': gain(ks[14], (n_odd, HEAD_DIM)),
        'd_q_gain': gain(ks[15], (n_odd, D_QK_DIM)),
        'd_k_gain': gain(ks[16], (n_odd, D_QK_DIM)),
        'd_lambda_q1': nrm(ks[17], (n_odd, D_QK_DIM), 0.1),
        'd_lambda_k1': nrm(ks[18], (n_odd, D_QK_DIM), 0.1),
        'd_lambda_q2': nrm(ks[19], (n_odd, D_QK_DIM), 0.1),
        'd_lambda_k2': nrm(ks[20], (n_odd, D_QK_DIM), 0.1),
        'd_subln_gain': gain(ks[21], (n_odd, D_V_DIM)),
        'ple_norm_gain': gain(ks[22], (DEPTH, D_MODEL)),
        'w_ple_gate': nrm(ks[23], (DEPTH, D_MODEL, D_MODEL), D_MODEL ** -0.5),
        'w_ple_proj': nrm(ks[24], (DEPTH, PLE_DIM, D_MODEL), PLE_DIM ** -0.5),
    }


def reference(x, p, norm_gain, w_in_even, w_out_even, a_q_gain, a_k_gain, idx_k_gain,
              b_q_gain, b_k_gain, b_sinks, w_in_odd, w_out_odd, c_q_gain, c_k_gain,
              d_q_gain, d_k_gain, d_lambda_q1, d_lambda_k1, d_lambda_q2, d_lambda_k2,
              d_subln_gain, ple_norm_gain, w_ple_gate, w_ple_proj):
    pos = jnp.arange(x.shape[1])
    for i in range(DEPTH):
        h = _rms_norm(x, norm_gain[i])
        j = i // 2
        if i % 2 == 0:
            y = _even_layer(h, pos, w_in_even[j], w_out_even[j], a_q_gain[j], a_k_gain[j],
                            idx_k_gain[j], b_q_gain[j], b_k_gain[j], b_sinks[j])
        else:
            lambda_init = 0.8 - 0.6 * math.exp(-0.3 * i)
            y = _odd_layer(h, pos, w_in_odd[j], w_out_odd[j], c_q_gain[j], c_k_gain[j],
                           d_q_gain[j], d_k_gain[j], d_lambda_q1[j], d_lambda_k1[j],
                           d_lambda_q2[j], d_lambda_k2[j], d_subln_gain[j], lambda_init)
        x = x + y
        gate = jax.nn.sigmoid(_rms_norm(x, ple_norm_gain[i]) @ w_ple_gate[i])
        x = x + (p[i] @ w_ple_proj[i]) * gate
    return x
```

```python
import contextlib
import numpy as np
import concourse.bass as bass
import concourse.mybir as mybir
from concourse.bass_utils import run_bass_kernel_spmd

F32 = mybir.dt.float32
BF16 = mybir.dt.bfloat16
ALU = mybir.AluOpType
AF = mybir.ActivationFunctionType
AX = mybir.AxisListType


STRICT = True


class _Rec:
    def __getattr__(self, name):
        def f(*a, **k):
            self.call = (name, a, k)
            return self
        return f


class Sched:
    ENG = ('pe', 'act', 'dve', 'pool', 'sp')
    CE = ('pe', 'act', 'dve', 'pool')

    def __init__(self, nc):
        self.nc = nc
        self.stack = contextlib.ExitStack()
        self.pstack = contextlib.ExitStack()
        self.semh = {}
        self.cnt = {}
        self.streams = {e: [] for e in self.ENG}
        self.waited = {e: {} for e in self.ENG}
        self.lastw = {}
        self.reads = {}
        self.final = []
        self.phase = 0
        self.dsem_free = []
        self.dsem_map = {}
        self.ndsem = 0
        self.nbar = 0
        for e in self.CE:
            self._sem('E_' + e)
        self._sem('BAR')

    def _sem(self, name):
        if name not in self.semh:
            self.semh[name] = self.stack.enter_context(self.nc.semaphore(name))
            self.cnt[name] = 0
        return self.semh[name]

    def _dsem(self, name):
        if name not in self.dsem_map:
            if self.dsem_free:
                iname = self.dsem_free.pop()
            else:
                iname = 'D_%d' % self.ndsem
                self.ndsem += 1
                self._sem(iname)
            self.dsem_map[name] = iname
        return self.dsem_map[name]

    def sb(self, name, shape, dtype):
        return self.pstack.enter_context(self.nc.sbuf_tensor("s%d_%s" % (self.phase, name), list(shape), dtype))

    def ps(self, name, shape, dtype):
        return self.pstack.enter_context(self.nc.psum_tensor("p%d_%s" % (self.phase, name), list(shape), dtype))

    def _deps(self, e, reads, writes):
        need = {}

        def add(dep, kind):
            sem, val, pe_ = dep
            if kind == 'WAW' and pe_ == 'dma' and getattr(self, '_nowaw', False):
                return
            if pe_ == e and e != 'sp':
                if e == 'pe':
                    return
                if not STRICT:
                    if kind != 'RAW' or val < self.cnt['E_' + e] - 1:
                        return
            if need.get(sem, 0) < val:
                need[sem] = val

        for k in reads:
            if k in self.lastw:
                add(self.lastw[k], 'RAW')
        for k in writes:
            if k in self.lastw:
                add(self.lastw[k], 'WAW')
            for d in self.reads.get(k, ()):
                add(d, 'WAR')
        waits = []
        for s, v in need.items():
            if self.waited[e].get(s, 0) < v:
                self.waited[e][s] = v
                waits.append((s, v))
        return waits

    def _record(self, ev, reads, writes):
        for k in writes:
            self.lastw[k] = ev
            self.reads[k] = []
        for k in reads:
            if k in writes:
                continue
            self.reads.setdefault(k, []).append(ev)

    def op(self, e, fn, reads=(), writes=(), inc=True):
        rec = _Rec()
        fn(rec)
        name_, a_, k_ = rec.call
        fn = lambda eng, name_=name_, a_=a_, k_=k_: getattr(eng, name_)(*a_, **k_)
        waits = self._deps(e, reads, writes)
        sem = 'E_' + e
        if inc:
            self.cnt[sem] += 1
            ev = (sem, self.cnt[sem], e)
            self.streams[e].append((waits, fn, sem, 1))
        else:
            assert e == 'pe'
            ev = (sem, self.cnt[sem] + 1, e)
            self.streams[e].append((waits, fn, None, 0))
        self._record(ev, reads, writes)

    def dma(self, q, out, in_, reads=(), writes=(), sem=None, final=False, nowaw=False):
        assert sem is not None
        sname = self._dsem(sem)
        self._nowaw = nowaw
        waits = self._deps(q, reads, writes)
        self._nowaw = False
        self.cnt[sname] += 16
        ev = (sname, self.cnt[sname], 'dma')
        self.streams[q].append((waits, lambda e, o=out, i=in_: e.dma_start(out=o, in_=i), sname, 16))
        self._record(ev, reads, writes)

    def allgather(self, in_ap, out_ap, groups, reads=()):
        sname = self._dsem('cc')
        waits = self._deps('pool', reads, ())
        self.cnt[sname] += 1
        self.streams['pool'].append((waits, lambda e: e.collective_compute(
            "AllGather", mybir.AluOpType.bypass, replica_groups=groups, ins=[in_ap], outs=[out_ap]), sname, 1))

    def end_phase(self):
        self.nbar += 1
        allsems = [(s, self.cnt[s]) for s in self.cnt if s.startswith('D_') and self.cnt[s] > 0]
        ce = [('E_' + x, self.cnt['E_' + x]) for x in self.CE if self.cnt['E_' + x] > 0]
        self.streams['sp'].append((allsems + ce, 'sem_inc', 'BAR', 1))
        for e in self.CE:
            w = [('BAR', self.nbar)] + [(s, v) for (s, v) in ce if s != 'E_' + e]
            self.streams[e].append((w, None, None, 0))
        for e in self.ENG:
            for (s, v) in ce + allsems:
                if self.waited[e].get(s, 0) < v:
                    self.waited[e][s] = v
        self.flush()
        self.pstack.close()
        self.pstack = contextlib.ExitStack()
        self.lastw = {}
        self.reads = {}
        for iname in self.dsem_map.values():
            self.dsem_free.append(iname)
        self.dsem_map = {}
        self.phase += 1

    def flush(self):
        nc = self.nc
        engs = {'pe': 'tensor', 'act': 'scalar', 'dve': 'vector', 'pool': 'gpsimd', 'sp': 'sync'}

        def replay(name, e):
            for waits, fn, sem, inc in self.streams[name]:
                for s, v in waits:
                    e.wait_ge(self.semh[s], v)
                if fn is None:
                    continue
                if fn == 'sem_inc':
                    e.sem_inc(self.semh[sem], inc)
                    continue
                ins = fn(e)
                if sem is not None:
                    ins.then_inc(self.semh[sem], inc)
            self.streams[name] = []

        with nc.Block() as block:
            for name in self.ENG:
                if not self.streams[name]:
                    continue
                getattr(block, engs[name])(lambda e, name=name: replay(name, e))

    def finish(self):
        self.stack.close()


class Ctx:
    def __init__(self, nc):
        self.nc = nc
        self.t = {}

    def get(self, name, shape, dtype, kind):
        if name not in self.t:
            self.t[name] = self.nc.dram_tensor(name, list(shape), dtype, kind=kind).ap()
        return self.t[name]

    def ein(self, name, shape, dtype):
        return self.get(name, shape, dtype, "ExternalInput")

    def mid(self, name, shape, dtype):
        return self.get(name, shape, dtype, "Internal")

    def eout(self, name, shape, dtype):
        return self.get(name, shape, dtype, "ExternalOutput")


class Arena:
    def __init__(self, S, nbytes):
        self.t = S.sb("arena", [128, nbytes // 2], BF16)
        self.cap = nbytes
        self.off = 0

    def alloc(self, shape, dtype):
        n = int(np.prod(shape))
        sz = n * (4 if dtype in (F32, mybir.dt.uint32, mybir.dt.int32) else 2)
        self.off = (self.off + 63) // 64 * 64
        assert self.off + sz <= self.cap, ("SBUF arena overflow", self.off, sz, self.cap)
        ap = self.t[:, self.off // 2:(self.off + sz) // 2]
        self.off += sz
        if dtype != BF16:
            ap = ap.bitcast(dtype)
        if len(shape) == 2:
            ap = ap.rearrange("p (a b) -> p a b", a=shape[0])
        elif len(shape) == 3:
            ap = ap.rearrange("p (a b c) -> p a b c", a=shape[0], b=shape[1])
        return ap


class Buf:
    _n = 0

    def __init__(self, ap, key=None):
        self.ap = ap
        Buf._n += 1
        self.k = key or ("b%d" % Buf._n)

    def __getitem__(self, idx):
        return self.ap[idx]


NT = 16
EPS = 1e-6
KF0 = 580
KF1 = 2048


def v3(ap, h):
    return ap.rearrange("p (h d) -> p h d", h=h)


class PBuilder:
    def __init__(self, ctx, S, layer):
        self.layer = layer
        nc = self.nc = ctx.nc
        self.S = S
        L = layer
        ncol = 3016 if L == 0 else 4096
        self.ncol = ncol
        self.x = ctx.ein("x", [128, NT * 1024], F32) if L == 0 else ctx.mid("x1", [128, NT * 1024], F32)
        self.w = ctx.ein("w_in%d" % L, [1024, ncol], F32)
        self.ng = ctx.ein("ng%d" % L, [128, 8], F32)
        self.gains = ctx.ein("gains%d" % L, [1, 5 * 64], F32)
        self.cs = ctx.ein("cs", [128, NT * 64], F32)
        self.identd = ctx.ein("ident", [128, 128], F32)
        self.ksides, self.kgs = kside_tensors(ctx, L)
        self.q1 = ctx.mid("q1_%d" % L, [128, NT * 512], BF16)
        self.q2 = ctx.mid("q2_%d" % L, [128, NT * 512], BF16)
        if L == 0:
            self.q3 = ctx.mid("q3_0", [128, NT * 512], BF16)
            self.iw = ctx.mid("iw", [128, NT * 8], F32)
        self.gates = ctx.mid("gates%d" % L, [128, NT * 1024], F32)
        self.build()
        S.end_phase()

    def build(self):
        S, nc, L = self.S, self.nc, self.layer
        sb = S.sb
        KF = KF0 if L == 0 else KF1
        tpc = TPC[L]
        if L == 0:
            segs = [(0, 512, 0), (640, 512, 512)]
            for j, h in enumerate([0, 4, 1, 5, 2, 6, 3, 7]):
                segs.append((1736 + h * 64, 64, 1024 + j * 64))
            segs += [(512, 64, 1536), (512, 64, 1600), (1152, 64, 1664), (1152, 64, 1728), (2248, 128, 1792),
                     (576, 64, 1920), (2376, 128, 1984), (1216, 8, 2112),
                     (1224, 512, 2120), (2504, 512, 2632)]
            ncolW = 3144
            qk = [(0, 512), (512, 512), (1024, 512), (1536, 384)]
            NH = 30
            gmap = [(0, 8, 0), (8, 16, None), (16, 24, 3), (24, 26, 1), (26, 28, 2), (28, 30, 4)]
            nonorm = (8, 16)
            gate_cols = (2120, 2632)
            NB = 15
        else:
            segs = [(i * 512, 512, i * 512) for i in range(8)]
            ncolW = 4096
            qk = [(0, 512), (512, 512), (2048, 512), (2560, 512)]
            NH = 32
            gmap = [(0, 8, 0), (8, 16, 1), (16, 24, 2), (24, 32, 3)]
            nonorm = None
            gate_cols = (1536, 3584)
            NB = 16
        ncol = self.ncol
        ident_f = sb("ident_f", [128, 128], F32)
        ident = sb("ident", [128, 128], BF16)
        S.dma('sp', ident_f[:], self.identd[:, :], writes=['ident_f'], sem='c0')
        S.op('dve', lambda e: e.tensor_copy(ident[:], ident_f[:]), reads=['ident_f'], writes=['ident'])
        ng = sb("ng", [128, 8], F32)
        S.dma('sp', ng[:], self.ng[:, :], writes=['ng'], sem='c1')
        gains = sb("gains", [128, 5 * 64], F32)
        S.dma('sp', gains[:], self.gains[0:1, :].partition_broadcast(128), writes=['gains'], sem='c2')
        cs = sb("cs", [128, NT * 64], F32)
        S.dma('sp', cs[:], self.cs[:, :], writes=['cs'], sem='c3')
        gtab = sb("gtab", [128, NH * 64], F32)
        for (h0, h1, gi) in gmap:
            dstv = gtab[:, h0 * 64:h1 * 64].rearrange("p (h d) -> p h d", d=64)
            if gi is None:
                S.op('dve', lambda e: e.memset(gtab[:, h0 * 64:h1 * 64], 1.0), writes=['gtab'])
            else:
                S.op('dve', lambda e: e.tensor_copy(dstv, gains[:, gi * 64:(gi + 1) * 64].unsqueeze(1).to_broadcast([128, h1 - h0, 64])),
                     reads=['gains'], writes=['gtab'])
        W = sb("W", [128, 8 * ncolW], BF16)
        wst = [sb("wst%d" % i, [128, ncol], F32) for i in range(2)]
        for c in range(8):
            b = c % 2
            S.dma('sp', wst[b][:], self.w[c * 128:(c + 1) * 128, :], writes=['wst%d' % b], sem='wst%d' % b)
            for i_, (s0, n, d0) in enumerate(segs):
                eng = 'act' if (n >= 512 and i_ % 2 == 0) or n < 512 else 'dve'
                if eng == 'act':
                    S.op('act', lambda e: e.activation(W[:, c * ncolW + d0:c * ncolW + d0 + n], wst[b][:, s0:s0 + n], AF.Copy,
                                                       scale=ng[:, c:c + 1]), reads=['wst%d' % b, 'ng'], writes=['W%d_%d' % (c, i_ % 2)])
                else:
                    S.op('dve', lambda e: e.tensor_scalar(W[:, c * ncolW + d0:c * ncolW + d0 + n], wst[b][:, s0:s0 + n],
                                                          ng[:, c:c + 1], None, ALU.mult), reads=['wst%d' % b, 'ng'], writes=['W%d_%d' % (c, i_ % 2)])
        xs = [sb("xs%d" % i, [128, 1024], F32) for i in range(3)]
        sqj = sb("sqj", [128, 1024], F32)
        ss = [sb("ss%d" % i, [128, 1], F32) for i in range(2)]
        rstd = [sb("rstd%d" % i, [128, 1], F32) for i in range(2)]
        hb = [sb("hb%d" % i, [128, 1024], BF16) for i in range(2)]
        hT = [sb("hT%d" % i, [128, 1024], BF16) for i in range(2)]
        pT = S.ps("pT", [128, 1024], BF16)
        psq = [S.ps("psq%d" % i, [128, 512], F32) for i in range(6)]
        pTq = S.ps("pTq", [128, 1024], BF16)
        sqa1 = sb("sqa0", [128, NH * 64], F32)
        sqa = [sqa1, sqa1]
        S.op('dve', lambda e: e.memset(sqa1[:], 0.0), writes=['sqa0'])
        ssh = [sb("ssh%d" % i, [128, NH], F32) for i in range(2)]
        xn = [sb("xn%d" % i, [128, NH * 64], F32) for i in range(2)]
        tmp = [sb("rt%d" % i, [128, NH * 32], F32) for i in range(4)]
        qn = [sb("qn%d" % i, [128, NH * 64], BF16) for i in range(2)]
        qst = [sb("qst%d" % i, [128, NB * 128], BF16) for i in range(2)]
        kst = [sb("kst%d" % i, [128, KF], BF16) for i in range(2)]
        gst = [sb("gst%d" % i, [128, 1024], F32) for i in range(2)]
        iwst = [sb("iwst%d" % i, [128, 8], F32) for i in range(2)]

        def xload(l):
            S.dma('sp', xs[l % 3][:], self.x[:, l * 1024:(l + 1) * 1024], writes=['xs%d' % (l % 3)], sem='xs%d' % (l % 3))

        def front(l):
            b = l % 2
            S.op('act', lambda e: e.activation(sqj[:], xs[l % 3][:], AF.Square, accum_out=ss[b][:]),
                 reads=['xs%d' % (l % 3)], writes=['sqj', 'ss%d' % b])
            S.op('dve', lambda e: e.tensor_scalar(rstd[b][:], ss[b][:], 1.0 / 1024, EPS, ALU.mult, ALU.add),
                 reads=['ss%d' % b], writes=['rstd%d' % b])
            S.op('act', lambda e: e.activation(rstd[b][:], rstd[b][:], AF.Ln), reads=['rstd%d' % b], writes=['rstd%d' % b])
            S.op('act', lambda e: e.activation(rstd[b][:], rstd[b][:], AF.Exp, scale=-0.5), reads=['rstd%d' % b], writes=['rstd%d' % b])
            S.op('act', lambda e: e.activation(hb[b][:], xs[l % 3][:], AF.Copy, scale=rstd[b][:]),
                 reads=['xs%d' % (l % 3), 'rstd%d' % b], writes=['hb%d' % b])

        def front_b(l):
            b = l % 2
            for c in range(8):
                S.op('pe', lambda e: e.transpose(pT[:, c * 128:(c + 1) * 128], hb[b][:, c * 128:(c + 1) * 128], ident[:]),
                     reads=['hb%d' % b, 'ident'], writes=['pT'], inc=(c == 7))
            S.op('dve', lambda e: e.tensor_copy(hT[l % 2][:], pT[:]), reads=['pT'], writes=['hT%d' % (l % 2)])

        def proj(l, bank, d0, n):
            h3 = l % 2
            for c in range(8):
                S.op('pe', lambda e: e.matmul(psq[bank][:, 0:n], hT[h3][:, c * 128:(c + 1) * 128],
                                              W[:, c * ncolW + d0:c * ncolW + d0 + n], start=(c == 0), stop=(c == 7)),
                     reads=['hT%d' % h3, 'W%d_0' % c, 'W%d_1' % c], writes=['psq%d' % bank], inc=(c == 7))

        def mid(l):
            b = l % 2
            kb, kk = kst[b], 'kst%d' % b
            hoff = 0
            for i, (d0, n) in enumerate(qk):
                proj(l, i, d0, n)
                nh = n // 64
                if not (nonorm and nonorm[0] == hoff):
                    S.op('act', lambda e: e.activation(sqa[b][:, hoff * 64:hoff * 64 + n], psq[i][:, 0:n], AF.Square),
                         reads=['psq%d' % i], writes=['sqa0'])
                hoff += nh

        def mid_b(l):
            b = l % 2
            kb, kk = kst[b], 'kst%d' % b
            S.op('dve', lambda e: e.tensor_reduce(ssh[b][:], sqa[b][:].rearrange("p (h d) -> p h d", d=64), AX.X, ALU.add),
                 reads=['sqa0'], writes=['ssh%d' % b])
            S.op('dve', lambda e: e.tensor_scalar(ssh[b][:], ssh[b][:], 1.0 / 64, EPS, ALU.mult, ALU.add),
                 reads=['ssh%d' % b], writes=['ssh%d' % b])
            S.op('act', lambda e: e.activation(ssh[b][:], ssh[b][:], AF.Ln), reads=['ssh%d' % b], writes=['ssh%d' % b])
            S.op('act', lambda e: e.activation(ssh[b][:], ssh[b][:], AF.Exp, scale=-0.5), reads=['ssh%d' % b], writes=['ssh%d' % b])
            if nonorm:
                S.op('dve', lambda e: e.memset(ssh[b][:, nonorm[0]:nonorm[1]], 1.0), writes=['ssh%d' % b])
            hoff = 0
            for i, (d0, n) in enumerate(qk):
                nh = n // 64
                S.op('dve', lambda e: e.tensor_tensor(
                    xn[b][:, hoff * 64:hoff * 64 + n].rearrange("p (h d) -> p h d", d=64),
                    psq[i][:, 0:n].rearrange("p (h d) -> p h d", d=64),
                    ssh[b][:, hoff:hoff + nh].unsqueeze(2).to_broadcast([128, nh, 64]), ALU.mult),
                    reads=['psq%d' % i, 'ssh%d' % b], writes=['xn%d' % b])
                hoff += nh
            if L == 0:
                proj(l, 4, 1920, 200)
                S.op('act', lambda e: e.activation(kb[:, 384:448], psq[4][:, 0:64], AF.Copy), reads=['psq4'], writes=[kk])
                S.op('act', lambda e: e.activation(kb[:, 449:513], psq[4][:, 64:128], AF.Copy), reads=['psq4'], writes=[kk])
                S.op('act', lambda e: e.activation(kb[:, 514:578], psq[4][:, 128:192], AF.Copy), reads=['psq4'], writes=[kk])
                S.op('act', lambda e: e.activation(iwst[b][:], psq[4][:, 192:200], AF.Copy), reads=['psq4'], writes=['iwst%d' % b])
                for cidx in (448, 513, 578):
                    S.op('pool', lambda e: e.memset(kb[:, cidx:cidx + 1], 1.0), writes=[kk])
                S.op('pool', lambda e: e.memset(kb[:, 579:580], 0.0), writes=[kk])
                S.dma('sp', self.iw[:, l * 8:(l + 1) * 8], iwst[b][:], reads=['iwst%d' % b], sem='iwst%d' % b)
            else:
                proj(l, 4, 1024, 512)
                S.op('act', lambda e: e.activation(kb[:, 512:1024], psq[4][:, 0:512], AF.Copy), reads=['psq4'], writes=[kk])
                proj(l, 5, 3072, 512)
                S.op('act', lambda e: e.activation(kb[:, 1536:2048], psq[5][:, 0:512], AF.Copy), reads=['psq5'], writes=[kk])
            for gi_, d0 in enumerate(gate_cols):
                gb_ = (5, 4)[gi_] if L == 0 else (4, 5)[gi_]
                proj(l, gb_, d0, 512)
                S.op('act', lambda e: e.activation(gst[b][:, gi_ * 512:(gi_ + 1) * 512], psq[gb_][:, 0:512], AF.Silu),
                     reads=['psq%d' % gb_], writes=['gst%d' % b])
            S.dma('sp', self.gates[:, l * 1024:(l + 1) * 1024], gst[b][:], reads=['gst%d' % b], sem='gst%d' % b)

        def back(l):
            b = l % 2
            kb, kk = kst[b], 'kst%d' % b
            xk = 'xn%d' % b
            x3 = xn[b][:].rearrange("p (h d) -> p h d", d=64)
            S.op('dve', lambda e: e.tensor_tensor(xn[b][:], xn[b][:], gtab[:], ALU.mult), reads=[xk, 'gtab'], writes=[xk])
            cosb = cs[:, l * 64:l * 64 + 32].unsqueeze(1).to_broadcast([128, NH, 32])
            sinb = cs[:, l * 64 + 32:l * 64 + 64].unsqueeze(1).to_broadcast([128, NH, 32])
            x1, x2 = x3[:, :, 0:32], x3[:, :, 32:64]
            t = [tt[:].rearrange("p (h d) -> p h d", d=32) for tt in tmp]
            o3 = qn[b][:].rearrange("p (h d) -> p h d", d=64)
            S.op('dve', lambda e: e.tensor_tensor(t[0], x1, cosb, ALU.mult), reads=[xk, 'cs'], writes=['rt0'])
            S.op('dve', lambda e: e.tensor_tensor(t[1], x2, sinb, ALU.mult), reads=[xk, 'cs'], writes=['rt1'])
            S.op('dve', lambda e: e.tensor_tensor(o3[:, :, 0:32], t[0], t[1], ALU.subtract), reads=['rt0', 'rt1'], writes=['qn%d' % b])
            S.op('dve', lambda e: e.tensor_tensor(t[2], x2, cosb, ALU.mult), reads=[xk, 'cs'], writes=['rt2'])
            S.op('dve', lambda e: e.tensor_tensor(t[3], x1, sinb, ALU.mult), reads=[xk, 'cs'], writes=['rt3'])
            S.op('dve', lambda e: e.tensor_tensor(o3[:, :, 32:64], t[2], t[3], ALU.add), reads=['rt2', 'rt3'], writes=['qn%d' % b])

        def back_b(l):
            b = l % 2
            kb, kk = kst[b], 'kst%d' % b
            for i in range(8):
                S.op('pe', lambda e: e.transpose(pTq[:, i * 128:(i + 1) * 128], qn[b][:, i * 128:(i + 1) * 128], ident[:]),
                     reads=['qn%d' % b, 'ident'], writes=['pTq'], inc=(i == 7))
            S.op('act', lambda e: e.activation(qst[b][:, 0:1024], pTq[:, 0:1024], AF.Copy), reads=['pTq'], writes=['qst%d' % b])
            for i in range(8, NB):
                S.op('pe', lambda e: e.transpose(pTq[:, (i - 8) * 128:(i - 7) * 128], qn[b][:, i * 128:(i + 1) * 128], ident[:]),
                     reads=['qn%d' % b, 'ident'], writes=['pTq'], inc=(i == NB - 1))
            S.op('act', lambda e: e.activation(qst[b][:, 1024:NB * 128], pTq[:, 0:(NB - 8) * 128], AF.Copy), reads=['pTq'], writes=['qst%d' % b])
            qs = 'qst%d' % b
            ksd = self.ksides[chunk_of(L, l)[0]][:, chunk_of(L, l)[1] * KF:(chunk_of(L, l)[1] + 1) * KF]
            if L == 0:
                S.dma('sp', self.q1[:, l * 512:(l + 1) * 512], qst[b][:, 0:512], reads=[qs], sem='qo1_%d' % b)
                S.dma('sp', self.q2[:, l * 512:(l + 1) * 512], qst[b][:, 512:1024], reads=[qs], sem='qo2_%d' % b)
                S.dma('sp', self.q3[:, l * 512:(l + 1) * 512], qst[b][:, 1024:1536], reads=[qs], sem='qo3_%d' % b)
                S.op('dve', lambda e: e.tensor_copy(kb[:, 0:384], qst[b][:, 1536:1920]), reads=[qs], writes=[kk])
            else:
                S.dma('sp', self.q1[:, l * 512:(l + 1) * 512], qst[b][:, 0:512], reads=[qs], sem='qo1_%d' % b)
                S.dma('sp', self.q2[:, l * 512:(l + 1) * 512], qst[b][:, 1024:1536], reads=[qs], sem='qo2_%d' % b)
                S.op('dve', lambda e: e.tensor_copy(kb[:, 0:512], qst[b][:, 512:1024]), reads=[qs], writes=[kk])
                S.op('dve', lambda e: e.tensor_copy(kb[:, 1024:1536], qst[b][:, 1536:2048]), reads=[qs], writes=[kk])
            S.dma('sp', ksd, kb[:], reads=[kk], writes=['ksd%d' % l], sem=kk)
            ci_, slot_ = chunk_of(L, l)
            if slot_ == tpc - 1:
                S.allgather(self.ksides[ci_][:, :], self.kgs[ci_][:, :], GROUPS, reads=['ksd%d' % t_ for t_ in chunk_tiles(L, ci_)])

        xload(0)
        xload(1)
        front(0)
        for t_ in range(NT + 2):
            if t_ + 2 < NT:
                xload(t_ + 2)
            if 0 <= t_ - 1 < NT:
                mid(t_ - 1)
            if 0 <= t_ - 2 < NT:
                back(t_ - 2)
            if t_ < NT:
                front_b(t_)
            if 0 <= t_ - 1 < NT:
                mid_b(t_ - 1)
            if t_ + 1 < NT:
                front(t_ + 1)
            if 0 <= t_ - 2 < NT:
                back_b(t_ - 2)


def gtile(c, l):
    r = c % 4
    return 4 * l + (r if l % 2 == 0 else 3 - r)


def own_rows(c):
    return np.stack([gtile(c, l) * 128 + np.arange(128) for l in range(NT)])


def tile_major(a2d):
    F_ = a2d.shape[1]
    return np.ascontiguousarray(a2d.reshape(NT, 128, F_).transpose(1, 0, 2).reshape(128, NT * F_))


def from_tile_major(a, F_):
    return a.reshape(128, NT, F_).transpose(1, 0, 2).reshape(NT * 128, F_)


def rope_tables(c):
    pos = own_rows(c).astype(np.float32)
    inv = (np.float32(10000.0) ** (-np.arange(32, dtype=np.float32) / np.float32(32))).astype(np.float32)
    ang = (pos[:, :, None] * inv[None, None, :]).astype(np.float32)
    t = np.concatenate([np.cos(ang), np.sin(ang)], axis=-1).astype(np.float32)
    return np.ascontiguousarray(t.transpose(1, 0, 2).reshape(128, NT * 64))


def kchunk(v):
    return np.ascontiguousarray(v.reshape(8, 128).T)


NEG = -1.0e30
DEBUG = False
NIT = 13

TPC = {0: 4, 1: 2}


def chunk_of(L, l):
    if L == 0:
        return l // 4, l % 4
    return 2 * (l // 4) + (l % 2), (l % 4) // 2


def chunk_tiles(L, ci):
    return [l for l in range(NT) if chunk_of(L, l)[0] == ci]


def kside_tensors(ctx, L):
    KF = KF0 if L == 0 else KF1
    tpc = TPC[L]
    ks = [ctx.mid("kside%d_%d" % (L, i), [128, tpc * KF], BF16) for i in range(NT // tpc)]
    kg = [ctx.mid("kg%d_%d" % (L, i), [512, tpc * KF], BF16) for i in range(NT // tpc)]
    return ks, kg


def load_kside_global(S, q_in, dst, kgs, L, f0, key, npad, H, dh, dw):
    KF = KF0 if L == 0 else KF1
    tpc = TPC[L]
    d4 = dst.rearrange("p (t h w) -> p t h w", h=H, w=dw)
    for ci in range(NT // tpc):
        tiles = chunk_tiles(L, ci)
        q = q_in[ci % len(q_in)] if isinstance(q_in, (list, tuple)) else q_in
        for rp in range(4):
            src = kgs[ci][rp * 128:(rp + 1) * 128, :].rearrange("p (t f) -> p t f", f=KF)
            for par in range(2):
                ls = [l for l in tiles if l % 2 == par]
                if not ls:
                    continue
                assert len(ls) == 2 and ls[1] == ls[0] + 2
                o = rp if par == 0 else 3 - rp
                pos = 4 * ls[0] + o + npad
                s0 = chunk_of(L, ls[0])[1]
                st = chunk_of(L, ls[1])[1] - s0
                if H == 1:
                    S.dma(q, d4[:, pos:pos + 9:8, 0, 0:dh], src[:, s0:s0 + st + 1:st, f0:f0 + dh], writes=[key], sem=key, nowaw=True)
                else:
                    for i_, l_ in enumerate(ls):
                        sl = chunk_of(L, l_)[1]
                        S.dma(q, d4[:, pos + 8 * i_, :, 0:dh], src[:, sl, f0:f0 + H * dh].rearrange("p (h d) -> p h d", h=H),
                              writes=[key], sem=key, nowaw=True)


class ABuilder:
    def __init__(self, ctx, S):
        nc = self.nc = ctx.nc
        self.S = S
        _, self.kg = kside_tensors(ctx, 0)
        self.q1 = ctx.mid("q1_0", [128, NT * 512], BF16)
        self.q2 = ctx.mid("q2_0", [128, NT * 512], BF16)
        self.iw = ctx.mid("iw", [128, NT * 8], F32)
        self.cbias = ctx.ein("cbias", [128, 2 * 512], F32)
        self.identd = ctx.ein("ident", [128, 128], F32)
        self.oa = ctx.mid("oa0", [128, NT * 512], F32)
        self.dbg = None
        self.build()
        S.end_phase()

    def build(self):
        S, nc = self.S, self.nc
        sb = S.sb
        ident_f = sb("ident_f", [128, 128], F32)
        ident = sb("ident", [128, 128], BF16)
        S.dma('sp', ident_f[:], self.identd[:, :], writes=['ident_f'], sem='c0')
        S.op('dve', lambda e: e.tensor_copy(ident[:], ident_f[:]), reads=['ident_f'], writes=['ident'])
        iw = sb("iw", [128, NT * 8], F32)
        S.dma('sp', iw[:], self.iw[:, :], writes=['iw'], sem='c1')
        cbias = sb("cbias", [128, 1024], F32)
        S.dma('sp', cbias[:], self.cbias[:, :], writes=['cbias'], sem='c2')
        akT = sb("akT", [128, 64 * 128], BF16)
        ikT = sb("ikT", [128, 64 * 128], BF16)
        av = sb("av", [128, 64 * 65], BF16)
        load_kside_global(S, 'sp', ikT[:], self.kg, 0, 128, 'ikT', 0, 1, 128, 128)
        load_kside_global(S, 'act', akT[:], self.kg, 0, 0, 'akT', 0, 1, 128, 128)
        load_kside_global(S, ('act', 'pool'), av[:], self.kg, 0, 384, 'av', 0, 1, 65, 65)
        score = [sb("score%d" % i, [128, 8192], F32) for i in range(2)]
        dtmp = [sb("dtmp%d" % i, [128, 512], F32) for i in range(2)]
        iq = [sb("iq%d" % i, [128, 512], BF16) for i in range(2)]
        aq = [sb("aq%d" % i, [128, 512], BF16) for i in range(2)]
        diagw = [sb("diagw%d" % i, [128, 1024], BF16) for i in range(2)]
        r = [sb("r%d" % i, [128, 512], BF16) for i in range(6)]
        mrow = [sb("mrow%d" % i, [128, 8192], BF16) for i in range(2)]
        mT = [sb("mT%d" % i, [128, 512], BF16) for i in range(2)]
        bT = [sb("bT%d" % i, [128, 1024], BF16) for i in range(2)]
        negb = sb("negb", [128, 1], F32)
        S.op('pool', lambda e: e.memset(negb[:], -30000.0), writes=['negb'])
        pT = [[sb("pT%d_%d" % (e, i), [128, 512], BF16) for i in range(3)] for e in range(2)]
        st = {n: [sb("%s%d" % (n, i), [128, 1], F32) for i in range(2)] for n in ('lo', 'hi', 'mid', 'cnt', 'm1', 'tt')}
        htab = [sb("htab%d" % i, [128, NIT + 1], F32) for i in range(2)]
        pw = sb("pw", [128, NIT + 1], F32)
        for it in range(NIT + 1):
            S.op('pool', lambda e: e.memset(pw[:, it:it + 1], 2.0 ** (-it)), writes=['pw'])
        iota8 = sb("iota8", [128, 8], F32)
        for i in range(8):
            S.op('pool', lambda e: e.memset(iota8[:, i:i + 1], float(i)), writes=['iota8'])
        dsc = sb("dsc", [128, 8192], F32)
        junk = dsc[:].bitcast(BF16)
        m8 = sb("m8", [128, 8], F32)
        oh = sb("oh", [128, 8], F32)
        geu = [sb("geu%d" % i, [128, 1], mybir.dt.uint32) for i in range(2)]
        ltu = [sb("ltu%d" % i, [128, 1], mybir.dt.uint32) for i in range(2)]
        rec = sb("rec", [128, 8], F32)
        oast1 = sb("oast0", [128, 512], F32)
        oast = [oast1, oast1]
        xps = [S.ps("xps%d" % i, [128, 512], F32) for i in range(2)]
        scps = S.ps("scps", [128, 512], F32)
        sps = [S.ps("sps%d" % i, [128, 512], F32) for i in range(2)]
        mTps = S.ps("mTps", [128, 1024], BF16)
        O = [S.ps("O%d" % i, [128, 512], F32) for i in range(2)]

        SEQ_A = list(range(0, NT, 2)) + list(range(NT - 1, 0, -2))
        ppar = {k_: i_ % 2 for i_, k_ in enumerate(SEQ_A)}

        def stage1(k):
            par = ppar[k]
            S.dma('sp', iq[par][:], self.q2[:, k * 512:(k + 1) * 512], writes=['iq%d' % par], sem='iq%d' % par)
            S.dma('sp', aq[ppar[k]][:], self.q1[:, k * 512:(k + 1) * 512], writes=['aq%d' % ppar[k]], sem='aq%d' % ppar[k])
            for h in range(8):
                S.op('act', lambda e: e.activation(diagw[par][:, h * 128:(h + 1) * 128], ident_f[:], AF.Copy,
                                                   scale=iw[:, k * 8 + h:k * 8 + h + 1]),
                     reads=['ident_f', 'iw'], writes=['diagw%d' % par])
            ri = [0]

            xb = [(xps[0], 'xps0'), (xps[1], 'xps1'), (sps[0], 'sps0'), (sps[1], 'sps1')]

            def xmm(c, h):
                e_, p_ = h % 2, h // 2
                xt_, xk_ = xb[h % 4]
                S.op('pe', lambda e: e.matmul(xt_[:], iq[par][64 * e_:64 * e_ + 64, p_ * 128:(p_ + 1) * 128],
                                              ikT[64 * e_:64 * e_ + 64, c * 512:(c + 1) * 512], start=True, stop=True),
                     reads=['iq%d' % par, 'ikT'], writes=[xk_])
                S.op('act', lambda e: e.activation(r[h % 6][:], xt_[:], AF.Relu), reads=[xk_], writes=['r%d' % (h % 6)])

            def dmm(c, h):
                S.op('pe', lambda e: e.matmul(scps[:], diagw[par][:, h * 128:(h + 1) * 128], r[h % 6][:],
                                              start=(h == 0), stop=(h == 7)),
                     reads=['diagw%d' % par, 'r%d' % (h % 6)], writes=['scps'])

            for c in range(k + 1):
                for h in range(3):
                    xmm(c, h)
                for h in range(3, 8):
                    xmm(c, h)
                    dmm(c, h - 3)
                for h in range(5, 8):
                    dmm(c, h)
                S.op('act', lambda e: e.activation(score[par][:, c * 512:(c + 1) * 512], scps[:], AF.Copy),
                     reads=['scps'], writes=['score%d' % par])
                yield
            cb = cbias[:, (k % 2) * 512:(k % 2 + 1) * 512]
            dg = score[par][:, k * 512:(k + 1) * 512]
            S.op('pool', lambda e: e.tensor_tensor(dtmp[par][:], dg, cb, ALU.subtract),
                 reads=['score%d' % par, 'cbias'], writes=['dtmp%d' % par])
            S.op('pool', lambda e: e.tensor_tensor(dg, dg, cb, ALU.add),
                 reads=['score%d' % par, 'cbias'], writes=['score%d' % par])

        def stage2(k):
            par = ppar[k]
            N = 512 * (k + 1)
            sc = score[par][:, 0:N]
            lo, hi, mid, cnt, m1 = (st[n][par] for n in ('lo', 'hi', 'mid', 'cnt', 'm1'))
            kl, kh, km, kc, k1 = ('%s%d' % (n, par) for n in ('lo', 'hi', 'mid', 'cnt', 'm1'))
            skey = 'score%d' % par
            ht, kht = htab[par], 'htab%d' % par
            tt, ktt = st['tt'][par], 'tt%d' % par
            S.op('dve', lambda e: e.tensor_reduce(hi[:], sc, AX.X, ALU.max), reads=[skey], writes=[kh])
            S.op('dve', lambda e: e.tensor_reduce(lo[:], dtmp[par][:], AX.X, ALU.min), reads=['dtmp%d' % par], writes=[kl])
            if k > 0:
                S.op('dve', lambda e: e.tensor_reduce(m1[:], score[par][:, 0:N - 512], AX.X, ALU.min), reads=[skey], writes=[k1])
                S.op('dve', lambda e: e.tensor_tensor(lo[:], lo[:], m1[:], ALU.min), reads=[kl, k1], writes=[kl])
            S.op('dve', lambda e: e.tensor_tensor(tt[:], hi[:], lo[:], ALU.subtract), reads=[kh, kl], writes=[ktt])
            S.op('dve', lambda e: e.tensor_scalar(mid[:], lo[:], hi[:, 0:1], 0.5, ALU.add, ALU.mult), reads=[kl, kh], writes=[km])
            S.op('dve', lambda e: e.tensor_scalar(ht[:], pw[:], tt[:, 0:1], 0.501, ALU.mult, ALU.mult), reads=['pw', ktt], writes=[kht])
            for it in range(NIT):
                S.op('dve', lambda e: e.tensor_scalar(junk[:, 0:N], sc, mid[:, 0:1], 0.0, ALU.is_ge, ALU.add, accum_out=cnt[:]),
                     reads=[skey, km], writes=['dsc', kc])
                S.op('dve', lambda e: e.scalar_tensor_tensor(tt[:], cnt[:], 255.5, ht[:, it:it + 1], ALU.is_ge, ALU.mult),
                     reads=[kc, kht], writes=[ktt])
                S.op('dve', lambda e: e.scalar_tensor_tensor(mid[:], tt[:], ht[:, it + 1:it + 2], mid[:], ALU.subtract, ALU.add),
                     reads=[ktt, kht, km], writes=[km])
            S.op('dve', lambda e: e.tensor_tensor(lo[:], mid[:], ht[:, NIT:NIT + 1], ALU.subtract), reads=[km, kht], writes=[kl])
            S.op('dve', lambda e: e.tensor_tensor(hi[:], mid[:], ht[:, NIT:NIT + 1], ALU.add), reads=[km, kht], writes=[kh])
            mk = 'mrow%d' % par
            S.op('dve', lambda e: e.tensor_scalar(mrow[par][:, 0:N], sc, hi[:, 0:1], 0.0, ALU.is_ge, ALU.add, accum_out=cnt[:]),
                 reads=[skey, kh], writes=[mk, kc])
            S.op('dve', lambda e: e.scalar_tensor_tensor(dsc[:, 0:N], mrow[par][:, 0:N], NEG, sc, ALU.mult, ALU.add),
                 reads=[skey, mk], writes=['dsc'])
            S.op('dve', lambda e: e.max(out=m8[:], in_=dsc[:, 0:N]), reads=['dsc'], writes=['m8'])
            S.op('dve', lambda e: e.tensor_scalar(m8[:], m8[:], lo[:, 0:1], None, ALU.subtract), reads=['m8', kl], writes=['m8'])
            S.op('dve', lambda e: e.tensor_scalar(tt[:], cnt[:], -1.0, 255.0, ALU.mult, ALU.add), reads=[kc], writes=[ktt])
            S.op('dve', lambda e: e.tensor_scalar(oh[:], iota8[:], tt[:, 0:1], None, ALU.is_equal), reads=['iota8', ktt], writes=['oh'])
            S.op('dve', lambda e: e.tensor_tensor(oh[:], oh[:], m8[:], ALU.mult), reads=['oh', 'm8'], writes=['oh'])
            S.op('dve', lambda e: e.tensor_reduce(tt[:], oh[:], AX.X, ALU.add), reads=['oh'], writes=[ktt])
            S.op('dve', lambda e: e.tensor_scalar(tt[:], tt[:], 0.0, None, ALU.max), reads=[ktt], writes=[ktt])
            S.op('dve', lambda e: e.tensor_tensor(lo[:], lo[:], tt[:], ALU.add), reads=[kl, ktt], writes=[kl])
            S.op('dve', lambda e: e.tensor_scalar(mrow[par][:, 0:N], sc, lo[:, 0:1], None, ALU.is_ge),
                 reads=[skey, kl], writes=['mrow%d' % par])

        def stage3(k):
            par = ppar[k]
            nkt = 4 * k + 4
            lo = st['lo'][par]
            kl = 'lo%d' % par
            skey = 'score%d' % par

            def smm(j):
                c, u = j // 4, j % 4
                for e_ in range(2):
                    pebias = (u == 3)
                    sp_, sk_ = sb4[(e_, j % 2)]
                    S.op('pe', lambda e: e.matmul(sp_[:], akT[64 * e_:64 * e_ + 64, j * 128:(j + 1) * 128],
                                                  aq[ppar[k]][64 * e_:64 * e_ + 64, :], start=True, stop=not pebias),
                         reads=['akT', 'aq%d' % ppar[k]], writes=[sk_], inc=not pebias)
                    if pebias:
                        S.op('pe', lambda e: e.matmul(sp_[:], ident[:], bT[c % 2][:, 0:512],
                                                      start=False, stop=True),
                             reads=['ident', 'bT%d' % (c % 2)], writes=[sk_])
                    pt = pT[e_][j % 3]
                    pk = 'pT%d_%d' % (e_, j % 3)
                    S.op('act', lambda e: e.activation(pt[:], sp_[:], AF.Exp, scale=0.125),
                         reads=[sk_], writes=[pk])
                    if pebias:
                        continue
                    S.op('pool', lambda e: e.tensor_tensor(
                        pt[:].rearrange("p (s q) -> p s q", s=4), pt[:].rearrange("p (s q) -> p s q", s=4),
                        mT[c % 2][:, u * 128:(u + 1) * 128].unsqueeze(1).to_broadcast([128, 4, 128]), ALU.mult),
                        reads=[pk, 'mT%d' % (c % 2)], writes=[pk])

            def pvmm(j):
                for e_ in range(2):
                    pt = pT[e_][j % 3]
                    pk = 'pT%d_%d' % (e_, j % 3)
                    for p_ in range(4):
                        h = 2 * p_ + e_
                        S.op('pe', lambda e: e.matmul(O[h // 4][:, (h % 4) * 65:(h % 4) * 65 + 65],
                                                      pt[:, p_ * 128:(p_ + 1) * 128], av[:, j * 65:(j + 1) * 65],
                                                      start=(j == 0 and h % 4 == 0), stop=(j == nkt - 1),
                                                      skip_group_check=True),
                             reads=[pk, 'av'], writes=['O%d' % (h // 4)], inc=(p_ == 3))

            if DEBUG and k < 2:
                S.dma('sp', self.dbg[:, k * 8200:k * 8200 + 8192], score[par][:], reads=[skey], sem='dbg', final=True)
                dbgst = sb("dbgst%d" % k, [128, 4], F32)
                for i_, n_ in enumerate(('lo', 'hi', 'mid', 'cnt')):
                    S.op('dve', lambda e: e.tensor_copy(dbgst[:, i_:i_ + 1], st[n_][par][:]), reads=['%s%d' % (n_, par)], writes=['dbgst%d' % k])
                S.dma('sp', self.dbg[:, k * 8200 + 8192:k * 8200 + 8196], dbgst[:], reads=['dbgst%d' % k], sem='dbg2', final=True)
            sb4 = {(0, 0): (sps[0], 'sps0'), (0, 1): (xps[0], 'xps0'), (1, 0): (sps[1], 'sps1'), (1, 1): (xps[1], 'xps1')}

            def prep(c):
                mc = c % 2
                for u in range(4):
                    S.op('pe', lambda e: e.transpose(mTps[:, mc * 512 + u * 128:mc * 512 + (u + 1) * 128],
                                                     mrow[par][:, c * 512 + u * 128:c * 512 + (u + 1) * 128], ident[:]),
                         reads=['mrow%d' % par, 'ident'], writes=['mTps%d' % mc], inc=(u == 3))
                S.op('act', lambda e: e.activation(mT[mc][:], mTps[:, mc * 512:(mc + 1) * 512], AF.Copy),
                     reads=['mTps%d' % mc], writes=['mT%d' % mc])
                for ui, u in enumerate((3,)):
                    S.op('act', lambda e: e.activation(
                        bT[mc][:, ui * 512:(ui + 1) * 512].rearrange("p (r q) -> p r q", r=4),
                        mTps[:, mc * 512 + u * 128:mc * 512 + (u + 1) * 128].unsqueeze(1).to_broadcast([128, 4, 128]),
                        AF.Identity, scale=30000.0, bias=negb[:, 0:1]),
                        reads=['mTps%d' % mc, 'negb'], writes=['bT%d' % mc])

            prep(0)
            for c in range(k + 1):
                for u in range(4):
                    j = 4 * c + u
                    smm(j)
                    if j > 1:
                        pvmm(j - 2)
                    if u == 3 and c + 1 <= k:
                        prep(c + 1)
                yield
            pvmm(nkt - 2)
            pvmm(nkt - 1)

        def stage3n(k):
            par = ppar[k]
            for bnk in range(2):
                Ov = O[bnk][:, 0:260].rearrange("p (h d) -> p h d", h=4)
                S.op('dve', lambda e: e.reciprocal(rec[:, bnk * 4:(bnk + 1) * 4], Ov[:, :, 64]),
                     reads=['O%d' % bnk], writes=['rec'])
                S.op('dve', lambda e: e.tensor_tensor(
                    oast[par][:, bnk * 256:(bnk + 1) * 256].rearrange("p (h d) -> p h d", h=4), Ov[:, :, 0:64],
                    rec[:, bnk * 4:(bnk + 1) * 4].unsqueeze(2).to_broadcast([128, 4, 64]), ALU.mult),
                    reads=['O%d' % bnk, 'rec'], writes=['oast0'])
            S.dma('pool', self.oa[:, k * 512:(k + 1) * 512], oast[par][:], reads=['oast0'], sem='oast0', final=True)

        def run(g):
            for _ in g:
                pass

        seq = SEQ_A
        run(stage1(seq[0]))
        stage2(seq[0])
        for i_, k in enumerate(seq):
            nxt = seq[i_ + 1] if i_ + 1 < NT else None
            if nxt is not None:
                run(stage1(nxt))
            run(stage3(k))
            if nxt is not None:
                stage2(nxt)
            stage3n(k)


def make_cbias(c):
    r = c % 4
    out = np.zeros((128, 2, 4, 128), np.float32)
    q = np.arange(128)[:, None]
    s = np.arange(128)[None, :]
    for par in range(2):
        o = r if par == 0 else 3 - r
        for u in range(4):
            if u == o:
                out[:, par, u, :] = np.where(s <= q, 0.0, NEG)
            elif u > o:
                out[:, par, u, :] = NEG
    return out.reshape(128, 1024)


def gather4(per_core):
    out = []
    for c in range(8):
        b = c // 4
        out.append(np.ascontiguousarray(np.concatenate([per_core[4 * b + rp] for rp in range(4)], axis=0)))
    return out


LAMBDA_INIT = 0.8 - 0.6 * float(np.exp(-0.3 * 1))

WCFG = {
    'B': dict(KF=KF0, kf=(256, 128), vf=(449, 130), npad=1, nm=5, shared=True, nv=65, nO=8, vH=1, vdh=130, vdw=130),
    'C': dict(KF=KF1, kf=(0, 512), vf=(512, 520), npad=16, nm=20, shared=False, nv=65, nO=8, vH=8, vdh=64, vdw=65),
    'D': dict(KF=KF1, kf=(1024, 512), vf=(1536, 516), npad=0, nm=4, shared=False, nv=129, nO=8, vH=4, vdh=128, vdw=129),
}


class WBuilder:
    def __init__(self, ctx, S, kind):
        self.kind = kind
        cfg = self.cfg = WCFG[kind]
        nc = self.nc = ctx.nc
        self.S = S
        L = 0 if kind == 'B' else 1
        self.L = L
        _, self.kg = kside_tensors(ctx, L)
        self.q = ctx.mid({'B': 'q3_0', 'C': 'q1_1', 'D': 'q2_1'}[kind], [128, NT * 512], BF16)
        self.maskd = ctx.ein("mask" + kind, [128, 2 * cfg['nm'] * 128], BF16)
        if kind == 'B':
            self.extra = ctx.ein("extraB", [1, 8], F32)
        if kind == 'D':
            self.extra = ctx.ein("extraD", [1, 4 * 64 + 128], F32)
        self.o = ctx.mid("o" + kind, [128, NT * 512], F32)
        self.build()
        S.end_phase()

    def build(self):
        S, nc, cfg, kind = self.S, self.nc, self.cfg, self.kind
        sb = S.sb
        KF, npad, nm, nv = cfg['KF'], cfg['npad'], cfg['nm'], cfg['nv']
        ntl = 64 + npad
        kw, vw = cfg['kf'][1], cfg['vf'][1]
        kall = sb("kall", [128, ntl * kw], BF16)
        vall = sb("vall", [128, ntl * vw], BF16)
        if npad:
            S.op('pool', lambda e: e.memset(kall[:, 0:npad * kw], 0.0), writes=['kall'])
            S.op('pool', lambda e: e.memset(vall[:, 0:npad * vw], 0.0), writes=['vall'])
        if cfg['vdw'] != cfg['vdh']:
            ones = vall[:, npad * vw:].rearrange("p (t h w) -> p t h w", h=cfg['vH'], w=cfg['vdw'])[:, :, :, cfg['vdh']:cfg['vdw']]
            S.op('pool', lambda e: e.memset(ones, 1.0), writes=['vall'])
        load_kside_global(S, 'sp', kall[:], self.kg, self.L, cfg['kf'][0], 'kall', npad, 1, kw, kw)
        load_kside_global(S, ('act', 'pool'), vall[:], self.kg, self.L, cfg['vf'][0], 'vall', npad, cfg['vH'], cfg['vdh'], cfg['vdw'])
        mask = sb("mask", [128, 2 * nm * 128], BF16)
        S.dma('sp', mask[:], self.maskd[:, :], writes=['mask'], sem='c0')
        qb = [sb("q%d" % i, [128, 512], BF16) for i in range(2)]
        pT = [[sb("pT%d_%d" % (e, i), [128, 512], BF16) for i in range(3)] for e in range(2)]
        rec = sb("rec", [128, 8], F32)
        ost = [sb("ost%d" % i, [128, 512], F32) for i in range(2)]
        sps = [S.ps("sps%d" % i, [128, 512], F32) for i in range(4)]
        nOb = 2 if nv == 65 else 3
        per_bank = 4 if nv == 65 else 3
        O = [S.ps("O%d" % i, [128, 512], F32) for i in range(nOb)]
        if kind == 'B':
            sk = sb("sinks", [128, 8], F32)
            S.dma('sp', sk[:], self.extra[0:1, :].partition_broadcast(128), writes=['sinks'], sem='c1')
            S.op('act', lambda e: e.activation(sk[:], sk[:], AF.Exp), reads=['sinks'], writes=['sinks'])
            den = sb("den", [128, 8], F32)
        if kind == 'D':
            ex = sb("extra", [128, 384], F32)
            S.dma('sp', ex[:], self.extra[0:1, :].partition_broadcast(128), writes=['extra'], sem='c1')
            lt = sb("lamtmp", [128, 128], F32)
            ls = sb("lams", [128, 2], F32)
            neglam = sb("neglam", [128, 1], F32)
            S.op('dve', lambda e: e.tensor_tensor(lt[:, 0:64], ex[:, 0:64], ex[:, 64:128], ALU.mult), reads=['extra'], writes=['lamtmp'])
            S.op('dve', lambda e: e.tensor_tensor(lt[:, 64:128], ex[:, 128:192], ex[:, 192:256], ALU.mult), reads=['extra'], writes=['lamtmp'])
            S.op('dve', lambda e: e.tensor_reduce(ls[:], lt[:].rearrange("p (a d) -> p a d", a=2), AX.X, ALU.add), reads=['lamtmp'], writes=['lams'])
            S.op('act', lambda e: e.activation(ls[:], ls[:], AF.Exp), reads=['lams'], writes=['lams'])
            S.op('dve', lambda e: e.tensor_tensor(neglam[:], ls[:, 1:2], ls[:, 0:1], ALU.subtract), reads=['lams'], writes=['neglam'])
            S.op('dve', lambda e: e.tensor_scalar(neglam[:], neglam[:], -LAMBDA_INIT, None, ALU.add), reads=['neglam'], writes=['neglam'])
            sg = sb("subg", [128, 128], F32)
            S.op('dve', lambda e: e.tensor_scalar(sg[:], ex[:, 256:384], 1.0 - LAMBDA_INIT, None, ALU.mult), reads=['extra'], writes=['subg'])
            t1 = sb("t1", [128, 128], F32)
            t2 = sb("t2", [128, 128], F32)
            sqj = sb("sqj", [128, 128], F32)
            ssd = sb("ssd", [128, 4], F32)

        def oloc(m):
            return m // per_bank, (m % per_bank) * nv

        Oc = [sb("Oc%d" % i, [128, nOb * 512], F32) for i in range(2)]
        pend = []

        def epilogue(k, par):
            okey = 'ost%d' % par
            ock = 'Oc%d' % par
            Os = lambda b_: Oc[par][:, b_ * 512:(b_ + 1) * 512]
            if kind in ('B', 'C'):
                for bnk in range(2):
                    Ov = Os(bnk)[:, 0:260].rearrange("p (h d) -> p h d", h=4)
                    rs = rec[:, bnk * 4:(bnk + 1) * 4]
                    if kind == 'B':
                        S.op('dve', lambda e: e.tensor_tensor(den[:, bnk * 4:(bnk + 1) * 4], Ov[:, :, 64], sk[:, bnk * 4:(bnk + 1) * 4], ALU.add),
                             reads=[ock, 'sinks'], writes=['den'])
                        S.op('dve', lambda e: e.reciprocal(rs, den[:, bnk * 4:(bnk + 1) * 4]), reads=['den'], writes=['rec'])
                    else:
                        S.op('dve', lambda e: e.reciprocal(rs, Ov[:, :, 64]), reads=[ock], writes=['rec'])
                    S.op('dve', lambda e: e.tensor_tensor(
                        ost[par][:, bnk * 256:(bnk + 1) * 256].rearrange("p (h d) -> p h d", h=4), Ov[:, :, 0:64],
                        rs.unsqueeze(2).to_broadcast([128, 4, 64]), ALU.mult),
                        reads=[ock, 'rec'], writes=[okey])
            else:
                for m in range(8):
                    bnk, off = oloc(m)
                    S.op('dve', lambda e: e.reciprocal(rec[:, m:m + 1], Os(bnk)[:, off + 128:off + 129]),
                         reads=[ock], writes=['rec'])
                for h in range(4):
                    b1, o1 = oloc(2 * h)
                    b2, o2 = oloc(2 * h + 1)
                    S.op('dve', lambda e: e.tensor_scalar(t1[:], Os(b1)[:, o1:o1 + 128], rec[:, 2 * h:2 * h + 1], None, ALU.mult),
                         reads=[ock, 'rec'], writes=['t1'])
                    S.op('dve', lambda e: e.tensor_scalar(t2[:], Os(b2)[:, o2:o2 + 128], rec[:, 2 * h + 1:2 * h + 2], None, ALU.mult),
                         reads=[ock, 'rec'], writes=['t2'])
                    S.op('dve', lambda e: e.scalar_tensor_tensor(ost[par][:, h * 128:(h + 1) * 128], t2[:], neglam[:, 0:1], t1[:],
                                                                 ALU.mult, ALU.add),
                         reads=['t1', 't2', 'neglam'], writes=[okey])
                    S.op('act', lambda e: e.activation(sqj[:], ost[par][:, h * 128:(h + 1) * 128], AF.Square, accum_out=ssd[:, h:h + 1]),
                         reads=[okey], writes=['sqj', 'ssd'])
                S.op('dve', lambda e: e.tensor_scalar(ssd[:], ssd[:], 1.0 / 128, EPS, ALU.mult, ALU.add), reads=['ssd'], writes=['ssd'])
                S.op('act', lambda e: e.activation(ssd[:], ssd[:], AF.Ln), reads=['ssd'], writes=['ssd'])
                S.op('act', lambda e: e.activation(ssd[:], ssd[:], AF.Exp, scale=-0.5), reads=['ssd'], writes=['ssd'])
                o3 = ost[par][:].rearrange("p (h d) -> p h d", h=4)
                S.op('dve', lambda e: e.tensor_tensor(o3, o3, ssd[:].unsqueeze(2).to_broadcast([128, 4, 128]), ALU.mult),
                     reads=[okey, 'ssd'], writes=[okey])
                S.op('dve', lambda e: e.tensor_tensor(o3, o3, sg[:].unsqueeze(1).to_broadcast([128, 4, 128]), ALU.mult),
                     reads=[okey, 'subg'], writes=[okey])
            S.dma('sp', self.o[:, k * 512:(k + 1) * 512], ost[par][:], reads=[okey], sem=okey, final=True)

        for k in range(NT):
            par = k % 2
            qk = 'q%d' % par
            S.dma('sp', qb[par][:], self.q[:, k * 512:(k + 1) * 512], writes=[qk], sem=qk)
            if kind == 'D':
                tiles = [(j, (j - 4 * k) if j >= 4 * k else None) for j in range(4 * k + 4)]
            else:
                tiles = [(4 * k + jj, jj) for jj in range(nm)]
            started = set()
            ntile = len(tiles)

            def smm(idx):
                pos, mi = tiles[idx]
                if cfg['shared']:
                    for e_ in range(2):
                        si = 2 * e_ + idx % 2
                        S.op('pe', lambda e: e.matmul(sps[si][:], kall[64 * e_:64 * e_ + 64, pos * kw:pos * kw + 128],
                                                      qb[par][64 * e_:64 * e_ + 64, :], start=True, stop=True),
                             reads=['kall', qk], writes=['sps%d' % si])
                else:
                    for p_ in range(4):
                        for e_ in range(2):
                            si = 2 * e_ + idx % 2
                            S.op('pe', lambda e: e.matmul(sps[si][:, p_ * 128:(p_ + 1) * 128],
                                                          kall[64 * e_:64 * e_ + 64, pos * kw + p_ * 128:pos * kw + (p_ + 1) * 128],
                                                          qb[par][64 * e_:64 * e_ + 64, p_ * 128:(p_ + 1) * 128], start=True, stop=True),
                                 reads=['kall', qk], writes=['sps%d' % si], inc=(p_ == 3))
                for e_ in range(2):
                    si = 2 * e_ + idx % 2
                    pt = pT[e_][idx % 3]
                    pk = 'pT%d_%d' % (e_, idx % 3)
                    S.op('act', lambda e: e.activation(pt[:], sps[si][:], AF.Exp, scale=0.125),
                         reads=['sps%d' % si], writes=[pk])
                    if mi is not None:
                        mo = (par * nm + mi) * 128
                        S.op('dve', lambda e: e.tensor_tensor(
                            pt[:].rearrange("p (s q) -> p s q", s=4), pt[:].rearrange("p (s q) -> p s q", s=4),
                            mask[:, mo:mo + 128].unsqueeze(1).to_broadcast([128, 4, 128]), ALU.mult),
                            reads=[pk, 'mask'], writes=[pk])

            def pvmm(idx):
                pos, mi = tiles[idx]
                for e_ in range(2):
                    pt = pT[e_][idx % 3]
                    pk = 'pT%d_%d' % (e_, idx % 3)
                    for p_ in range(4):
                        if kind == 'B':
                            m = 4 * e_ + p_
                            vo = pos * vw + e_ * 65
                        elif kind == 'C':
                            m = 2 * p_ + e_
                            vo = pos * vw + m * 65
                        else:
                            m = 2 * p_ + e_
                            vo = pos * vw + p_ * 129
                        bnk, off = oloc(m)
                        st_ = bnk not in started
                        started.add(bnk)
                        S.op('pe', lambda e: e.matmul(O[bnk][:, off:off + nv], pt[:, p_ * 128:(p_ + 1) * 128],
                                                      vall[:, vo:vo + nv], start=st_, stop=(idx == ntile - 1),
                                                      skip_group_check=True),
                             reads=[pk, 'vall'], writes=['O%d' % bnk], inc=(p_ == 3))

            for idx in range(ntile):
                smm(idx)
                if idx > 1:
                    pvmm(idx - 2)
                if idx == min(3, ntile - 1) and pend:
                    epilogue(*pend.pop())
            pvmm(ntile - 2)
            pvmm(ntile - 1)
            used = per_bank * nv
            for bnk in range(nOb):
                S.op('dve', lambda e: e.tensor_copy(Oc[par][:, bnk * 512:bnk * 512 + used], O[bnk][:, 0:used]),
                     reads=['O%d' % bnk], writes=['Oc%d' % par])
            pend.append((k, par))
        epilogue(*pend.pop())


def make_wmask(c, kind):
    r = c % 4
    nm = WCFG[kind]['nm']
    out = np.zeros((128, 2, nm, 128), np.float32)
    s = np.arange(128)[:, None]
    q = np.arange(128)[None, :]
    for par in range(2):
        o = r if par == 0 else 3 - r
        for jj in range(nm):
            if kind == 'B':
                dt = o + 1 - jj
            elif kind == 'C':
                dt = o + 16 - jj
            else:
                dt = o - jj
            dl = 128 * dt + q - s
            if kind == 'B':
                m = ((dl >= 0) & (dl < 128)).astype(np.float32)
            elif kind == 'D':
                m = (dl >= 0).astype(np.float32)
            else:
                m = ((dl >= 0) & (dl <= 128)).astype(np.float32) \
                    + ((dl >= 0) & (dl <= 512) & (dl % 4 == 0)).astype(np.float32) \
                    + ((dl >= 0) & (dl <= 2048) & (dl % 16 == 0)).astype(np.float32)
            out[:, par, jj, :] = m
    import ml_dtypes
    return out.reshape(128, 2 * nm * 128).astype(ml_dtypes.bfloat16)


class PostBuilder:
    def __init__(self, ctx, S, layer):
        nc = self.nc = ctx.nc
        self.S = S
        L = layer
        self.oa = ctx.mid("oa0" if L == 0 else "oC", [128, NT * 512], F32)
        self.ob = ctx.mid("oB" if L == 0 else "oD", [128, NT * 512], F32)
        self.gates = ctx.mid("gates%d" % L, [128, NT * 1024], F32)
        self.x = ctx.ein("x", [128, NT * 1024], F32) if L == 0 else ctx.mid("x1", [128, NT * 1024], F32)
        self.p = ctx.ein("p%d" % L, [128, NT * 256], F32)
        self.w_out = ctx.ein("w_out%d" % L, [1024, 1024], F32)
        self.w_gate = ctx.ein("w_gate%d" % L, [1024, 1024], F32)
        self.w_proj = ctx.ein("w_proj%d" % L, [256, 1024], F32)
        self.pg = ctx.ein("pg%d" % L, [128, 8], F32)
        self.identd = ctx.ein("ident", [128, 128], F32)
        self.xo = ctx.mid("x1", [128, NT * 1024], F32) if L == 0 else ctx.eout("y", [128, NT * 1024], F32)
        self.build()
        S.end_phase()

    def build(self):
        S, nc = self.S, self.nc
        sb = S.sb
        ident_f = sb("ident_f", [128, 128], F32)
        ident = sb("ident", [128, 128], BF16)
        S.dma('sp', ident_f[:], self.identd[:, :], writes=['ident_f'], sem='c0')
        S.op('dve', lambda e: e.tensor_copy(ident[:], ident_f[:]), reads=['ident_f'], writes=['ident'])
        pg = sb("pg", [128, 8], F32)
        S.dma('sp', pg[:], self.pg[:, :], writes=['pg'], sem='c1')
        wo = sb("wo", [128, 8 * 1024], BF16)
        wg = sb("wg", [128, 8 * 1024], BF16)
        wp = sb("wp", [128, 2 * 1024], BF16)
        wst = [sb("wst%d" % i, [128, 1024], F32) for i in range(4)]
        wi = 0
        for wi_m, (src, dst, nch, scale) in enumerate(((self.w_out, wo, 8, False), (self.w_gate, wg, 8, True), (self.w_proj, wp, 2, False))):
            for c in range(nch):
                b = wi % 4
                wi += 1
                S.dma(('sp', 'pool')[b % 2], wst[b][:], src[c * 128:(c + 1) * 128, :], writes=['wst%d' % b], sem='wst%d' % b)
                wkey = 'w%d_%d' % (wi_m, c)
                if scale:
                    S.op('act', lambda e: e.activation(dst[:, c * 1024:(c + 1) * 1024], wst[b][:], AF.Copy, scale=pg[:, c:c + 1]),
                         reads=['wst%d' % b, 'pg'], writes=[wkey])
                else:
                    S.op('dve', lambda e: e.tensor_copy(dst[:, c * 1024:(c + 1) * 1024], wst[b][:]),
                         reads=['wst%d' % b], writes=[wkey])
        ot = [sb("ot%d" % i, [128, 1024], F32) for i in range(2)]
        gt = [sb("gt%d" % i, [128, 1024], F32) for i in range(2)]
        xt = [sb("xt%d" % i, [128, 1024], F32) for i in range(3)]
        ptl = [sb("ptl%d" % i, [128, 256], F32) for i in range(2)]
        y = [sb("y%d" % i, [128, 1024], BF16) for i in range(2)]
        yT = [sb("yT%d" % i, [128, 1024], BF16) for i in range(2)]
        x1 = [sb("x1_%d" % i, [128, 1024], F32) for i in range(3)]
        sqj = sb("sqj", [128, 1024], F32)
        ss = sb("ss", [128, 1], F32)
        hb = sb("hb", [128, 1024], BF16)
        hT = [sb("hT%d" % i, [128, 1024], BF16) for i in range(3)]
        sig = sb("sig", [128, 1024], F32)
        pb = sb("pb", [128, 256], BF16)
        ppT = [sb("ppT%d" % i, [128, 256], BF16) for i in range(4)]
        tmp = sb("tmp", [128, 1024], F32)
        xo = [sb("xo%d" % i, [128, 1024], F32) for i in range(2)]
        tp = [S.ps("tp%d" % i, [128, 1024], BF16) for i in range(2)]
        mo = [S.ps("mo%d" % i, [128, 512], F32) for i in range(2)]
        mg = [S.ps("mg%d" % i, [128, 512], F32) for i in range(2)]
        mp = [S.ps("mp%d" % i, [128, 512], F32) for i in range(2)]

        def loads(k):
            b = k % 2
            S.dma('sp', ot[b][:, 0:512], self.oa[:, k * 512:(k + 1) * 512], writes=['ot%d' % b], sem='ota%d' % b)
            S.dma('sp', ot[b][:, 512:1024], self.ob[:, k * 512:(k + 1) * 512], writes=['ot%d' % b], sem='otb%d' % b, nowaw=True)
            S.dma('sp', gt[b][:], self.gates[:, k * 1024:(k + 1) * 1024], writes=['gt%d' % b], sem='gt%d' % b)
            S.dma('sp', xt[k % 3][:], self.x[:, k * 1024:(k + 1) * 1024], writes=['xt%d' % (k % 3)], sem='xt%d' % (k % 3))
            S.dma('sp', ptl[b][:], self.p[:, k * 256:(k + 1) * 256], writes=['ptl%d' % b], sem='ptl%d' % b)

        def stA(k):
            b = k % 2
            S.op('dve', lambda e: e.tensor_tensor(y[b][:], ot[b][:], gt[b][:], ALU.mult), reads=['ot%d' % b, 'gt%d' % b], writes=['y%d' % b])
            for c in range(8):
                S.op('pe', lambda e: e.transpose(tp[0][:, c * 128:(c + 1) * 128], y[b][:, c * 128:(c + 1) * 128], ident[:]),
                     reads=['y%d' % b, 'ident'], writes=['tp0'], inc=(c == 7))
            S.op('act', lambda e: e.activation(yT[b][:], tp[0][:], AF.Copy), reads=['tp0'], writes=['yT%d' % b])
            S.op('pool', lambda e: e.tensor_copy(pb[:], ptl[b][:]), reads=['ptl%d' % b], writes=['pb'])
            for c in range(2):
                S.op('pe', lambda e: e.transpose(tp[0][:, c * 128:(c + 1) * 128], pb[:, c * 128:(c + 1) * 128], ident[:]),
                     reads=['pb', 'ident'], writes=['tp0'], inc=(c == 1))
            S.op('act', lambda e: e.activation(ppT[k % 4][:], tp[0][:, 0:256], AF.Copy), reads=['tp0'], writes=['ppT%d' % (k % 4)])

        def stB(k):
            b = k % 2
            for h in range(2):
                for c in range(8):
                    S.op('pe', lambda e: e.matmul(mo[h][:], yT[b][:, c * 128:(c + 1) * 128], wo[:, c * 1024 + h * 512:c * 1024 + (h + 1) * 512],
                                                  start=(c == 0), stop=(c == 7)), reads=['yT%d' % b, 'w0_%d' % c], writes=['mo%d' % h], inc=(c == 7))
                S.op('dve', lambda e: e.tensor_tensor(x1[k % 3][:, h * 512:(h + 1) * 512], mo[h][:], xt[k % 3][:, h * 512:(h + 1) * 512], ALU.add),
                     reads=['mo%d' % h, 'xt%d' % (k % 3)], writes=['x1_%d' % (k % 3)])
            S.op('act', lambda e: e.activation(sqj[:], x1[k % 3][:], AF.Square, accum_out=ss[:]), reads=['x1_%d' % (k % 3)], writes=['sqj', 'ss'])
            S.op('dve', lambda e: e.tensor_scalar(ss[:], ss[:], 1.0 / 1024, EPS, ALU.mult, ALU.add), reads=['ss'], writes=['ss'])
            S.op('act', lambda e: e.activation(ss[:], ss[:], AF.Ln), reads=['ss'], writes=['ss'])
            S.op('act', lambda e: e.activation(ss[:], ss[:], AF.Exp, scale=-0.5), reads=['ss'], writes=['ss'])
            S.op('act', lambda e: e.activation(hb[:], x1[k % 3][:], AF.Copy, scale=ss[:]), reads=['x1_%d' % (k % 3), 'ss'], writes=['hb'])

        def stB2(k):
            b = k % 2
            for c in range(8):
                S.op('pe', lambda e: e.transpose(tp[1][:, c * 128:(c + 1) * 128], hb[:, c * 128:(c + 1) * 128], ident[:]),
                     reads=['hb', 'ident'], writes=['tp1'], inc=(c == 7))
            S.op('dve', lambda e: e.tensor_copy(hT[k % 3][:], tp[1][:]), reads=['tp1'], writes=['hT%d' % (k % 3)])

        def stC(k):
            b = k % 2
            for h in range(2):
                for c in range(8):
                    S.op('pe', lambda e: e.matmul(mg[h][:], hT[k % 3][:, c * 128:(c + 1) * 128], wg[:, c * 1024 + h * 512:c * 1024 + (h + 1) * 512],
                                                  start=(c == 0), stop=(c == 7)), reads=['hT%d' % (k % 3), 'w1_%d' % c], writes=['mg%d' % h], inc=(c == 7))
                S.op('act', lambda e: e.activation(sig[:, h * 512:(h + 1) * 512], mg[h][:], AF.Sigmoid), reads=['mg%d' % h], writes=['sig'])
            for h in range(2):
                for c in range(2):
                    S.op('pe', lambda e: e.matmul(mp[h][:], ppT[k % 4][:, c * 128:(c + 1) * 128], wp[:, c * 1024 + h * 512:c * 1024 + (h + 1) * 512],
                                                  start=(c == 0), stop=(c == 1)), reads=['ppT%d' % (k % 4), 'w2_%d' % c], writes=['mp%d' % h], inc=(c == 1))
                S.op('dve', lambda e: e.tensor_tensor(tmp[:, h * 512:(h + 1) * 512], mp[h][:], sig[:, h * 512:(h + 1) * 512], ALU.mult),
                     reads=['mp%d' % h, 'sig'], writes=['tmp'])
            S.op('pool', lambda e: e.tensor_tensor(xo[b][:], x1[k % 3][:], tmp[:], ALU.add), reads=['x1_%d' % (k % 3), 'tmp'], writes=['xo%d' % b])
            S.dma('sp', self.xo[:, k * 1024:(k + 1) * 1024], xo[b][:], reads=['xo%d' % b], sem='xo%d' % b, final=True)

        loads(0)
        for t in range(NT + 3):
            if t + 1 < NT:
                loads(t + 1)
            if 0 <= t - 3 < NT:
                stC(t - 3)
            if 0 <= t - 1 < NT:
                stB(t - 1)
            if t < NT:
                stA(t)
            if 0 <= t - 1 < NT:
                stB2(t - 1)


GROUPS = [[0, 1, 2, 3], [4, 5, 6, 7]]
_prog = {}


def gather_phase(ctx, S, L):
    ks, kg = kside_tensors(ctx, L)
    for a, b in zip(ks, kg):
        S.allgather(a[:, :], b[:, :], GROUPS)
    S.end_phase()


def build_fused():
    nc = bass.Bass("TRN2", target_bir_lowering=False)
    S = Sched(nc)
    ctx = Ctx(nc)
    PBuilder(ctx, S, 0)
    ABuilder(ctx, S)
    WBuilder(ctx, S, 'B')
    PostBuilder(ctx, S, 0)
    PBuilder(ctx, S, 1)
    WBuilder(ctx, S, 'C')
    WBuilder(ctx, S, 'D')
    PostBuilder(ctx, S, 1)
    S.finish()
    return nc


def kernel(**inp):
    g = lambda k: np.asarray(inp[k])
    x, p = g('x'), g('p')
    if 'nc' not in _prog:
        _prog['nc'] = build_fused()
    nc = _prog['nc']
    ident = np.eye(128, dtype=np.float32)

    def gains(lst):
        o = np.zeros((1, 5 * 64), np.float32)
        for i, v in enumerate(lst):
            o[0, i * 64:(i + 1) * 64] = v
        return o

    shared = {
        "w_in0": np.ascontiguousarray(g('w_in_even')[0]), "w_in1": np.ascontiguousarray(g('w_in_odd')[0]),
        "ng0": kchunk(g('norm_gain')[0]), "ng1": kchunk(g('norm_gain')[1]),
        "gains0": gains([g('a_q_gain')[0], g('a_k_gain')[0], g('idx_k_gain')[0], g('b_q_gain')[0], g('b_k_gain')[0]]),
        "gains1": gains([g('c_q_gain')[0], g('c_k_gain')[0], g('d_q_gain')[0], g('d_k_gain')[0]]),
        "ident": ident,
        "extraB": np.ascontiguousarray(g('b_sinks').reshape(1, 8)).astype(np.float32),
        "extraD": np.concatenate([g('d_lambda_q1')[0], g('d_lambda_k1')[0], g('d_lambda_q2')[0], g('d_lambda_k2')[0],
                                  g('d_subln_gain')[0]]).astype(np.float32).reshape(1, 384),
        "w_out0": np.ascontiguousarray(g('w_out_even')[0]), "w_out1": np.ascontiguousarray(g('w_out_odd')[0]),
        "w_gate0": np.ascontiguousarray(g('w_ple_gate')[0]), "w_gate1": np.ascontiguousarray(g('w_ple_gate')[1]),
        "w_proj0": np.ascontiguousarray(g('w_ple_proj')[0]), "w_proj1": np.ascontiguousarray(g('w_ple_proj')[1]),
        "pg0": kchunk(g('ple_norm_gain')[0]), "pg1": kchunk(g('ple_norm_gain')[1]),
    }
    in_maps = []
    rows = [own_rows(c).reshape(-1) for c in range(8)]
    for c in range(8):
        m = dict(shared)
        m["x"] = tile_major(x[c // 4][rows[c]])
        m["p0"] = tile_major(p[0][c // 4][rows[c]])
        m["p1"] = tile_major(p[1][c // 4][rows[c]])
        m["cs"] = rope_tables(c)
        m["cbias"] = make_cbias(c)
        for kind in 'BCD':
            m["mask" + kind] = make_wmask(c, kind)
        in_maps.append(m)
    res = run_bass_kernel_spmd(nc, in_maps, core_ids=list(range(8)))
    out = np.zeros((2, 8192, 1024), np.float32)
    for c in range(8):
        out[c // 4][rows[c]] = from_tile_major(np.asarray(res.results[c]['y']), 1024)
    return out
```

```python
import contextlib
import numpy as np
import concourse.bass as bass
import concourse.mybir as mybir
from concourse.bass_utils import run_bass_kernel_spmd

F32 = mybir.dt.float32
BF16 = mybir.dt.bfloat16
ALU = mybir.AluOpType
AF = mybir.ActivationFunctionType
AX = mybir.AxisListType


STRICT = False


class _Rec:
    def __getattr__(self, name):
        def f(*a, **k):
            self.call = (name, a, k)
            return self
        return f


class Sched:
    ENG = ('pe', 'act', 'dve', 'pool', 'sp')
    CE = ('pe', 'act', 'dve', 'pool')

    def __init__(self, nc):
        self.nc = nc
        self.stack = contextlib.ExitStack()
        self.pstack = contextlib.ExitStack()
        self.semh = {}
        self.cnt = {}
        self.streams = {e: [] for e in self.ENG}
        self.waited = {e: {} for e in self.ENG}
        self.lastw = {}
        self.reads = {}
        self.final = []
        self.phase = 0
        self.dsem_free = []
        self.dsem_map = {}
        self.ndsem = 0
        self.nbar = 0
        for e in self.CE:
            self._sem('E_' + e)
        self._sem('BAR')

    def _sem(self, name):
        if name not in self.semh:
            self.semh[name] = self.stack.enter_context(self.nc.semaphore(name))
            self.cnt[name] = 0
        return self.semh[name]

    def _dsem(self, name):
        if name not in self.dsem_map:
            if self.dsem_free:
                iname = self.dsem_free.pop()
            else:
                iname = 'D_%d' % self.ndsem
                self.ndsem += 1
                self._sem(iname)
            self.dsem_map[name] = iname
        return self.dsem_map[name]

    def sb(self, name, shape, dtype):
        return self.pstack.enter_context(self.nc.sbuf_tensor("s%d_%s" % (self.phase, name), list(shape), dtype))

    def ps(self, name, shape, dtype):
        return self.pstack.enter_context(self.nc.psum_tensor("p%d_%s" % (self.phase, name), list(shape), dtype))

    def _deps(self, e, reads, writes):
        need = {}

        def add(dep, kind):
            sem, val, pe_ = dep
            if kind == 'WAW' and pe_ == 'dma' and getattr(self, '_nowaw', False):
                return
            if pe_ == e and e != 'sp':
                if e == 'pe':
                    return
                if not STRICT:
                    if kind != 'RAW' or val < self.cnt['E_' + e] - 1:
                        return
            if need.get(sem, 0) < val:
                need[sem] = val

        for k in reads:
            if k in self.lastw:
                add(self.lastw[k], 'RAW')
        for k in writes:
            if k in self.lastw:
                add(self.lastw[k], 'WAW')
            for d in self.reads.get(k, ()):
                add(d, 'WAR')
        waits = []
        for s, v in need.items():
            if self.waited[e].get(s, 0) < v:
                self.waited[e][s] = v
                waits.append((s, v))
        return waits

    def _record(self, ev, reads, writes):
        for k in writes:
            self.lastw[k] = ev
            self.reads[k] = []
        for k in reads:
            if k in writes:
                continue
            self.reads.setdefault(k, []).append(ev)

    def op(self, e, fn, reads=(), writes=(), inc=True):
        rec = _Rec()
        fn(rec)
        name_, a_, k_ = rec.call
        fn = lambda eng, name_=name_, a_=a_, k_=k_: getattr(eng, name_)(*a_, **k_)
        waits = self._deps(e, reads, writes)
        sem = 'E_' + e
        if inc:
            self.cnt[sem] += 1
            ev = (sem, self.cnt[sem], e)
            self.streams[e].append((waits, fn, sem, 1))
        else:
            assert e == 'pe'
            ev = (sem, self.cnt[sem] + 1, e)
            self.streams[e].append((waits, fn, None, 0))
        self._record(ev, reads, writes)

    def dma(self, q, out, in_, reads=(), writes=(), sem=None, final=False, nowaw=False):
        assert sem is not None
        sname = self._dsem(sem)
        self._nowaw = nowaw
        waits = self._deps(q, reads, writes)
        self._nowaw = False
        self.cnt[sname] += 16
        ev = (sname, self.cnt[sname], 'dma')
        self.streams[q].append((waits, lambda e, o=out, i=in_: e.dma_start(out=o, in_=i), sname, 16))
        self._record(ev, reads, writes)

    def allgather(self, in_ap, out_ap, groups, reads=()):
        sname = self._dsem('cc')
        waits = self._deps('pool', reads, ())
        self.cnt[sname] += 1
        self.streams['pool'].append((waits, lambda e: e.collective_compute(
            "AllGather", mybir.AluOpType.bypass, replica_groups=groups, ins=[in_ap], outs=[out_ap]), sname, 1))

    def end_phase(self):
        self.nbar += 1
        allsems = [(s, self.cnt[s]) for s in self.cnt if s.startswith('D_') and self.cnt[s] > 0]
        ce = [('E_' + x, self.cnt['E_' + x]) for x in self.CE if self.cnt['E_' + x] > 0]
        self.streams['sp'].append((allsems + ce, 'sem_inc', 'BAR', 1))
        for e in self.CE:
            w = [('BAR', self.nbar)] + [(s, v) for (s, v) in ce if s != 'E_' + e]
            self.streams[e].append((w, None, None, 0))
        for e in self.ENG:
            for (s, v) in ce + allsems:
                if self.waited[e].get(s, 0) < v:
                    self.waited[e][s] = v
        self.flush()
        self.pstack.close()
        self.pstack = contextlib.ExitStack()
        self.lastw = {}
        self.reads = {}
        for iname in self.dsem_map.values():
            self.dsem_free.append(iname)
        self.dsem_map = {}
        self.phase += 1

    def flush(self):
        nc = self.nc
        engs = {'pe': 'tensor', 'act': 'scalar', 'dve': 'vector', 'pool': 'gpsimd', 'sp': 'sync'}

        def replay(name, e):
            for waits, fn, sem, inc in self.streams[name]:
                for s, v in waits:
                    e.wait_ge(self.semh[s], v)
                if fn is None:
                    continue
                if fn == 'sem_inc':
                    e.sem_inc(self.semh[sem], inc)
                    continue
                ins = fn(e)
                if sem is not None:
                    ins.then_inc(self.semh[sem], inc)
            self.streams[name] = []

        with nc.Block() as block:
            for name in self.ENG:
                if not self.streams[name]:
                    continue
                getattr(block, engs[name])(lambda e, name=name: replay(name, e))

    def finish(self):
        self.stack.close()


class Ctx:
    def __init__(self, nc):
        self.nc = nc
        self.t = {}

    def get(self, name, shape, dtype, kind):
        if name not in self.t:
            self.t[name] = self.nc.dram_tensor(name, list(shape), dtype, kind=kind).ap()
        return self.t[name]

    def ein(self, name, shape, dtype):
        return self.get(name, shape, dtype, "ExternalInput")

    def mid(self, name, shape, dtype):
        return self.get(name, shape, dtype, "Internal")

    def eout(self, name, shape, dtype):
        return self.get(name, shape, dtype, "ExternalOutput")


class Arena:
    def __init__(self, S, nbytes):
        self.t = S.sb("arena", [128, nbytes // 2], BF16)
        self.cap = nbytes
        self.off = 0

    def alloc(self, shape, dtype):
        n = int(np.prod(shape))
        sz = n * (4 if dtype in (F32, mybir.dt.uint32, mybir.dt.int32) else 2)
        self.off = (self.off + 63) // 64 * 64
        assert self.off + sz <= self.cap, ("SBUF arena overflow", self.off, sz, self.cap)
        ap = self.t[:, self.off // 2:(self.off + sz) // 2]
        self.off += sz
        if dtype != BF16:
            ap = ap.bitcast(dtype)
        if len(shape) == 2:
            ap = ap.rearrange("p (a b) -> p a b", a=shape[0])
        elif len(shape) == 3:
            ap = ap.rearrange("p (a b c) -> p a b c", a=shape[0], b=shape[1])
        return ap


class Buf:
    _n = 0

    def __init__(self, ap, key=None):
        self.ap = ap
        Buf._n += 1
        self.k = key or ("b%d" % Buf._n)

    def __getitem__(self, idx):
        return self.ap[idx]


NT = 16
EPS = 1e-6
KF0 = 580
KF1 = 2048


def v3(ap, h):
    return ap.rearrange("p (h d) -> p h d", h=h)


class PBuilder:
    def __init__(self, ctx, S, layer):
        self.layer = layer
        nc = self.nc = ctx.nc
        self.S = S
        L = layer
        ncol = 3016 if L == 0 else 4096
        self.ncol = ncol
        self.x = ctx.ein("x", [128, NT * 1024], F32) if L == 0 else ctx.mid("x1", [128, NT * 1024], F32)
        self.w = ctx.ein("w_in%d" % L, [1024, ncol], F32)
        self.ng = ctx.ein("ng%d" % L, [128, 8], F32)
        self.gains = ctx.ein("gains%d" % L, [1, 5 * 64], F32)
        self.cs = ctx.ein("cs", [128, NT * 64], F32)
        self.identd = ctx.ein("ident", [128, 128], F32)
        self.ksides, self.kgs = kside_tensors(ctx, L)
        self.q1 = ctx.mid("q1_%d" % L, [128, NT * 512], BF16)
        self.q2 = ctx.mid("q2_%d" % L, [128, NT * 512], BF16)
        if L == 0:
            self.q3 = ctx.mid("q3_0", [128, NT * 512], BF16)
            self.iw = ctx.mid("iw", [128, NT * 8], F32)
        self.gates = ctx.mid("gates%d" % L, [128, NT * 1024], F32)
        self.build()
        S.end_phase()

    def build(self):
        S, nc, L = self.S, self.nc, self.layer
        sb = S.sb
        KF = KF0 if L == 0 else KF1
        tpc = TPC[L]
        if L == 0:
            segs = [(0, 512, 0), (640, 512, 512)]
            for j, h in enumerate([0, 4, 1, 5, 2, 6, 3, 7]):
                segs.append((1736 + h * 64, 64, 1024 + j * 64))
            segs += [(512, 64, 1536), (512, 64, 1600), (1152, 64, 1664), (1152, 64, 1728), (2248, 128, 1792),
                     (576, 64, 1920), (2376, 128, 1984), (1216, 8, 2112),
                     (1224, 512, 2120), (2504, 512, 2632)]
            ncolW = 3144
            qk = [(0, 512), (512, 512), (1024, 512), (1536, 384)]
            NH = 30
            gmap = [(0, 8, 0), (8, 16, None), (16, 24, 3), (24, 26, 1), (26, 28, 2), (28, 30, 4)]
            nonorm = (8, 16)
            gate_cols = (2120, 2632)
            NB = 15
        else:
            segs = [(i * 512, 512, i * 512) for i in range(8)]
            ncolW = 4096
            qk = [(0, 512), (512, 512), (2048, 512), (2560, 512)]
            NH = 32
            gmap = [(0, 8, 0), (8, 16, 1), (16, 24, 2), (24, 32, 3)]
            nonorm = None
            gate_cols = (1536, 3584)
            NB = 16
        ncol = self.ncol
        ident_f = sb("ident_f", [128, 128], F32)
        ident = sb("ident", [128, 128], BF16)
        S.dma('sp', ident_f[:], self.identd[:, :], writes=['ident_f'], sem='c0')
        S.op('dve', lambda e: e.tensor_copy(ident[:], ident_f[:]), reads=['ident_f'], writes=['ident'])
        ng = sb("ng", [128, 8], F32)
        S.dma('sp', ng[:], self.ng[:, :], writes=['ng'], sem='c1')
        gains = sb("gains", [128, 5 * 64], F32)
        S.dma('sp', gains[:], self.gains[0:1, :].partition_broadcast(128), writes=['gains'], sem='c2')
        cs = sb("cs", [128, NT * 64], F32)
        S.dma('sp', cs[:], self.cs[:, :], writes=['cs'], sem='c3')
        gtab = sb("gtab", [128, NH * 64], F32)
        for (h0, h1, gi) in gmap:
            dstv = gtab[:, h0 * 64:h1 * 64].rearrange("p (h d) -> p h d", d=64)
            if gi is None:
                S.op('dve', lambda e: e.memset(gtab[:, h0 * 64:h1 * 64], 1.0), writes=['gtab'])
            else:
                S.op('dve', lambda e: e.tensor_copy(dstv, gains[:, gi * 64:(gi + 1) * 64].unsqueeze(1).to_broadcast([128, h1 - h0, 64])),
                     reads=['gains'], writes=['gtab'])
        W = sb("W", [128, 8 * ncolW], BF16)
        wst = [sb("wst%d" % i, [128, ncol], F32) for i in range(2)]
        for c in range(8):
            b = c % 2
            S.dma('sp', wst[b][:], self.w[c * 128:(c + 1) * 128, :], writes=['wst%d' % b], sem='wst%d' % b)
            for i_, (s0, n, d0) in enumerate(segs):
                eng = 'act' if (n >= 512 and i_ % 2 == 0) or n < 512 else 'dve'
                if eng == 'act':
                    S.op('act', lambda e: e.activation(W[:, c * ncolW + d0:c * ncolW + d0 + n], wst[b][:, s0:s0 + n], AF.Copy,
                                                       scale=ng[:, c:c + 1]), reads=['wst%d' % b, 'ng'], writes=['W%d_%d' % (c, i_ % 2)])
                else:
                    S.op('dve', lambda e: e.tensor_scalar(W[:, c * ncolW + d0:c * ncolW + d0 + n], wst[b][:, s0:s0 + n],
                                                          ng[:, c:c + 1], None, ALU.mult), reads=['wst%d' % b, 'ng'], writes=['W%d_%d' % (c, i_ % 2)])
        xs = [sb("xs%d" % i, [128, 1024], F32) for i in range(3)]
        sqj = sb("sqj", [128, 1024], F32)
        ss = [sb("ss%d" % i, [128, 1], F32) for i in range(2)]
        rstd = [sb("rstd%d" % i, [128, 1], F32) for i in range(2)]
        hb = [sb("hb%d" % i, [128, 1024], BF16) for i in range(2)]
        hT = [sb("hT%d" % i, [128, 1024], BF16) for i in range(2)]
        pT = S.ps("pT", [128, 1024], BF16)
        psq = [S.ps("psq%d" % i, [128, 512], F32) for i in range(6)]
        pTq = S.ps("pTq", [128, 1024], BF16)
        sqa1 = sb("sqa0", [128, NH * 64], F32)
        sqa = [sqa1, sqa1]
        S.op('dve', lambda e: e.memset(sqa1[:], 0.0), writes=['sqa0'])
        ssh = [sb("ssh%d" % i, [128, NH], F32) for i in range(2)]
        xn = [sb("xn%d" % i, [128, NH * 64], F32) for i in range(2)]
        tmp = [sb("rt%d" % i, [128, NH * 32], F32) for i in range(4)]
        qn = [sb("qn%d" % i, [128, NH * 64], BF16) for i in range(2)]
        qst = [sb("qst%d" % i, [128, NB * 128], BF16) for i in range(2)]
        kst = [sb("kst%d" % i, [128, KF], BF16) for i in range(2)]
        gst = [sb("gst%d" % i, [128, 1024], F32) for i in range(2)]
        iwst = [sb("iwst%d" % i, [128, 8], F32) for i in range(2)]

        def xload(l):
            S.dma('sp', xs[l % 3][:], self.x[:, l * 1024:(l + 1) * 1024], writes=['xs%d' % (l % 3)], sem='xs%d' % (l % 3))

        def front(l):
            b = l % 2
            S.op('act', lambda e: e.activation(sqj[:], xs[l % 3][:], AF.Square, accum_out=ss[b][:]),
                 reads=['xs%d' % (l % 3)], writes=['sqj', 'ss%d' % b])
            S.op('dve', lambda e: e.tensor_scalar(rstd[b][:], ss[b][:], 1.0 / 1024, EPS, ALU.mult, ALU.add),
                 reads=['ss%d' % b], writes=['rstd%d' % b])
            S.op('act', lambda e: e.activation(rstd[b][:], rstd[b][:], AF.Ln), reads=['rstd%d' % b], writes=['rstd%d' % b])
            S.op('act', lambda e: e.activation(rstd[b][:], rstd[b][:], AF.Exp, scale=-0.5), reads=['rstd%d' % b], writes=['rstd%d' % b])
            S.op('act', lambda e: e.activation(hb[b][:], xs[l % 3][:], AF.Copy, scale=rstd[b][:]),
                 reads=['xs%d' % (l % 3), 'rstd%d' % b], writes=['hb%d' % b])

        def front_b(l):
            b = l % 2
            for c in range(8):
                S.op('pe', lambda e: e.transpose(pT[:, c * 128:(c + 1) * 128], hb[b][:, c * 128:(c + 1) * 128], ident[:]),
                     reads=['hb%d' % b, 'ident'], writes=['pT'], inc=(c == 7))
            S.op('dve', lambda e: e.tensor_copy(hT[l % 2][:], pT[:]), reads=['pT'], writes=['hT%d' % (l % 2)])

        def proj(l, bank, d0, n):
            h3 = l % 2
            for c in range(8):
                S.op('pe', lambda e: e.matmul(psq[bank][:, 0:n], hT[h3][:, c * 128:(c + 1) * 128],
                                              W[:, c * ncolW + d0:c * ncolW + d0 + n], start=(c == 0), stop=(c == 7)),
                     reads=['hT%d' % h3, 'W%d_0' % c, 'W%d_1' % c], writes=['psq%d' % bank], inc=(c == 7))

        def mid(l):
            b = l % 2
            kb, kk = kst[b], 'kst%d' % b
            hoff = 0
            for i, (d0, n) in enumerate(qk):
                proj(l, i, d0, n)
                nh = n // 64
                if not (nonorm and nonorm[0] == hoff):
                    S.op('act', lambda e: e.activation(sqa[b][:, hoff * 64:hoff * 64 + n], psq[i][:, 0:n], AF.Square),
                         reads=['psq%d' % i], writes=['sqa0'])
                hoff += nh

        def mid_b(l):
            b = l % 2
            kb, kk = kst[b], 'kst%d' % b
            S.op('dve', lambda e: e.tensor_reduce(ssh[b][:], sqa[b][:].rearrange("p (h d) -> p h d", d=64), AX.X, ALU.add),
                 reads=['sqa0'], writes=['ssh%d' % b])
            S.op('dve', lambda e: e.tensor_scalar(ssh[b][:], ssh[b][:], 1.0 / 64, EPS, ALU.mult, ALU.add),
                 reads=['ssh%d' % b], writes=['ssh%d' % b])
            S.op('act', lambda e: e.activation(ssh[b][:], ssh[b][:], AF.Ln), reads=['ssh%d' % b], writes=['ssh%d' % b])
            S.op('act', lambda e: e.activation(ssh[b][:], ssh[b][:], AF.Exp, scale=-0.5), reads=['ssh%d' % b], writes=['ssh%d' % b])
            if nonorm:
                S.op('dve', lambda e: e.memset(ssh[b][:, nonorm[0]:nonorm[1]], 1.0), writes=['ssh%d' % b])
            hoff = 0
            for i, (d0, n) in enumerate(qk):
                nh = n // 64
                S.op('dve', lambda e: e.tensor_tensor(
                    xn[b][:, hoff * 64:hoff * 64 + n].rearrange("p (h d) -> p h d", d=64),
                    psq[i][:, 0:n].rearrange("p (h d) -> p h d", d=64),
                    ssh[b][:, hoff:hoff + nh].unsqueeze(2).to_broadcast([128, nh, 64]), ALU.mult),
                    reads=['psq%d' % i, 'ssh%d' % b], writes=['xn%d' % b])
                hoff += nh
            if L == 0:
                proj(l, 4, 1920, 200)
                S.op('act', lambda e: e.activation(kb[:, 384:448], psq[4][:, 0:64], AF.Copy), reads=['psq4'], writes=[kk])
                S.op('act', lambda e: e.activation(kb[:, 449:513], psq[4][:, 64:128], AF.Copy), reads=['psq4'], writes=[kk])
                S.op('act', lambda e: e.activation(kb[:, 514:578], psq[4][:, 128:192], AF.Copy), reads=['psq4'], writes=[kk])
                S.op('act', lambda e: e.activation(iwst[b][:], psq[4][:, 192:200], AF.Copy), reads=['psq4'], writes=['iwst%d' % b])
                for cidx in (448, 513, 578):
                    S.op('pool', lambda e: e.memset(kb[:, cidx:cidx + 1], 1.0), writes=[kk])
                S.op('pool', lambda e: e.memset(kb[:, 579:580], 0.0), writes=[kk])
                S.dma('sp', self.iw[:, l * 8:(l + 1) * 8], iwst[b][:], reads=['iwst%d' % b], sem='iwst%d' % b)
            else:
                proj(l, 4, 1024, 512)
                S.op('act', lambda e: e.activation(kb[:, 512:1024], psq[4][:, 0:512], AF.Copy), reads=['psq4'], writes=[kk])
                proj(l, 5, 3072, 512)
                S.op('act', lambda e: e.activation(kb[:, 1536:2048], psq[5][:, 0:512], AF.Copy), reads=['psq5'], writes=[kk])
            for gi_, d0 in enumerate(gate_cols):
                gb_ = (5, 4)[gi_] if L == 0 else (4, 5)[gi_]
                proj(l, gb_, d0, 512)
                S.op('act', lambda e: e.activation(gst[b][:, gi_ * 512:(gi_ + 1) * 512], psq[gb_][:, 0:512], AF.Silu),
                     reads=['psq%d' % gb_], writes=['gst%d' % b])
            S.dma('sp', self.gates[:, l * 1024:(l + 1) * 1024], gst[b][:], reads=['gst%d' % b], sem='gst%d' % b)

        def back(l):
            b = l % 2
            kb, kk = kst[b], 'kst%d' % b
            xk = 'xn%d' % b
            x3 = xn[b][:].rearrange("p (h d) -> p h d", d=64)
            S.op('dve', lambda e: e.tensor_tensor(xn[b][:], xn[b][:], gtab[:], ALU.mult), reads=[xk, 'gtab'], writes=[xk])
            cosb = cs[:, l * 64:l * 64 + 32].unsqueeze(1).to_broadcast([128, NH, 32])
            sinb = cs[:, l * 64 + 32:l * 64 + 64].unsqueeze(1).to_broadcast([128, NH, 32])
            x1, x2 = x3[:, :, 0:32], x3[:, :, 32:64]
            t = [tt[:].rearrange("p (h d) -> p h d", d=32) for tt in tmp]
            o3 = qn[b][:].rearrange("p (h d) -> p h d", d=64)
            S.op('dve', lambda e: e.tensor_tensor(t[0], x1, cosb, ALU.mult), reads=[xk, 'cs'], writes=['rt0'])
            S.op('dve', lambda e: e.tensor_tensor(t[1], x2, sinb, ALU.mult), reads=[xk, 'cs'], writes=['rt1'])
            S.op('dve', lambda e: e.tensor_tensor(o3[:, :, 0:32], t[0], t[1], ALU.subtract), reads=['rt0', 'rt1'], writes=['qn%d' % b])
            S.op('dve', lambda e: e.tensor_tensor(t[2], x2, cosb, ALU.mult), reads=[xk, 'cs'], writes=['rt2'])
            S.op('dve', lambda e: e.tensor_tensor(t[3], x1, sinb, ALU.mult), reads=[xk, 'cs'], writes=['rt3'])
            S.op('dve', lambda e: e.tensor_tensor(o3[:, :, 32:64], t[2], t[3], ALU.add), reads=['rt2', 'rt3'], writes=['qn%d' % b])

        def back_b(l):
            b = l % 2
            kb, kk = kst[b], 'kst%d' % b
            for i in range(8):
                S.op('pe', lambda e: e.transpose(pTq[:, i * 128:(i + 1) * 128], qn[b][:, i * 128:(i + 1) * 128], ident[:]),
                     reads=['qn%d' % b, 'ident'], writes=['pTq'], inc=(i == 7))
            S.op('act', lambda e: e.activation(qst[b][:, 0:1024], pTq[:, 0:1024], AF.Copy), reads=['pTq'], writes=['qst%d' % b])
            for i in range(8, NB):
                S.op('pe', lambda e: e.transpose(pTq[:, (i - 8) * 128:(i - 7) * 128], qn[b][:, i * 128:(i + 1) * 128], ident[:]),
                     reads=['qn%d' % b, 'ident'], writes=['pTq'], inc=(i == NB - 1))
            S.op('act', lambda e: e.activation(qst[b][:, 1024:NB * 128], pTq[:, 0:(NB - 8) * 128], AF.Copy), reads=['pTq'], writes=['qst%d' % b])
            qs = 'qst%d' % b
            ksd = self.ksides[chunk_of(L, l)[0]][:, chunk_of(L, l)[1] * KF:(chunk_of(L, l)[1] + 1) * KF]
            if L == 0:
                S.dma('sp', self.q1[:, l * 512:(l + 1) * 512], qst[b][:, 0:512], reads=[qs], sem='qo1_%d' % b)
                S.dma('sp', self.q2[:, l * 512:(l + 1) * 512], qst[b][:, 512:1024], reads=[qs], sem='qo2_%d' % b)
                S.dma('sp', self.q3[:, l * 512:(l + 1) * 512], qst[b][:, 1024:1536], reads=[qs], sem='qo3_%d' % b)
                S.op('dve', lambda e: e.tensor_copy(kb[:, 0:384], qst[b][:, 1536:1920]), reads=[qs], writes=[kk])
            else:
                S.dma('sp', self.q1[:, l * 512:(l + 1) * 512], qst[b][:, 0:512], reads=[qs], sem='qo1_%d' % b)
                S.dma('sp', self.q2[:, l * 512:(l + 1) * 512], qst[b][:, 1024:1536], reads=[qs], sem='qo2_%d' % b)
                S.op('dve', lambda e: e.tensor_copy(kb[:, 0:512], qst[b][:, 512:1024]), reads=[qs], writes=[kk])
                S.op('dve', lambda e: e.tensor_copy(kb[:, 1024:1536], qst[b][:, 1536:2048]), reads=[qs], writes=[kk])
            S.dma('sp', ksd, kb[:], reads=[kk], writes=['ksd%d' % l], sem=kk)
            ci_, slot_ = chunk_of(L, l)
            if slot_ == tpc - 1:
                S.allgather(self.ksides[ci_][:, :], self.kgs[ci_][:, :], GROUPS, reads=['ksd%d' % t_ for t_ in chunk_tiles(L, ci_)])

        xload(0)
        xload(1)
        front(0)
        for t_ in range(NT + 2):
            if t_ + 2 < NT:
                xload(t_ + 2)
            if 0 <= t_ - 1 < NT:
                mid(t_ - 1)
            if 0 <= t_ - 2 < NT:
                back(t_ - 2)
            if t_ < NT:
                front_b(t_)
            if 0 <= t_ - 1 < NT:
                mid_b(t_ - 1)
            if t_ + 1 < NT:
                front(t_ + 1)
            if 0 <= t_ - 2 < NT:
                back_b(t_ - 2)


def gtile(c, l):
    r = c % 4
    return 4 * l + (r if l % 2 == 0 else 3 - r)


def own_rows(c):
    return np.stack([gtile(c, l) * 128 + np.arange(128) for l in range(NT)])


def tile_major(a2d):
    F_ = a2d.shape[1]
    return np.ascontiguousarray(a2d.reshape(NT, 128, F_).transpose(1, 0, 2).reshape(128, NT * F_))


def from_tile_major(a, F_):
    return a.reshape(128, NT, F_).transpose(1, 0, 2).reshape(NT * 128, F_)


def rope_tables(c):
    pos = own_rows(c).astype(np.float32)
    inv = (np.float32(10000.0) ** (-np.arange(32, dtype=np.float32) / np.float32(32))).astype(np.float32)
    ang = (pos[:, :, None] * inv[None, None, :]).astype(np.float32)
    t = np.concatenate([np.cos(ang), np.sin(ang)], axis=-1).astype(np.float32)
    return np.ascontiguousarray(t.transpose(1, 0, 2).reshape(128, NT * 64))


def kchunk(v):
    return np.ascontiguousarray(v.reshape(8, 128).T)


NEG = -1.0e30
DEBUG = False
NIT = 13

TPC = {0: 4, 1: 2}


def chunk_of(L, l):
    if L == 0:
        return l // 4, l % 4
    return 2 * (l // 4) + (l % 2), (l % 4) // 2


def chunk_tiles(L, ci):
    return [l for l in range(NT) if chunk_of(L, l)[0] == ci]


def kside_tensors(ctx, L):
    KF = KF0 if L == 0 else KF1
    tpc = TPC[L]
    ks = [ctx.mid("kside%d_%d" % (L, i), [128, tpc * KF], BF16) for i in range(NT // tpc)]
    kg = [ctx.mid("kg%d_%d" % (L, i), [512, tpc * KF], BF16) for i in range(NT // tpc)]
    return ks, kg


def load_kside_global(S, q_in, dst, kgs, L, f0, key, npad, H, dh, dw):
    KF = KF0 if L == 0 else KF1
    tpc = TPC[L]
    d4 = dst.rearrange("p (t h w) -> p t h w", h=H, w=dw)
    for ci in range(NT // tpc):
        tiles = chunk_tiles(L, ci)
        q = q_in[ci % len(q_in)] if isinstance(q_in, (list, tuple)) else q_in
        for rp in range(4):
            src = kgs[ci][rp * 128:(rp + 1) * 128, :].rearrange("p (t f) -> p t f", f=KF)
            for par in range(2):
                ls = [l for l in tiles if l % 2 == par]
                if not ls:
                    continue
                assert len(ls) == 2 and ls[1] == ls[0] + 2
                o = rp if par == 0 else 3 - rp
                pos = 4 * ls[0] + o + npad
                s0 = chunk_of(L, ls[0])[1]
                st = chunk_of(L, ls[1])[1] - s0
                if H == 1:
                    S.dma(q, d4[:, pos:pos + 9:8, 0, 0:dh], src[:, s0:s0 + st + 1:st, f0:f0 + dh], writes=[key], sem=key, nowaw=True)
                else:
                    for i_, l_ in enumerate(ls):
                        sl = chunk_of(L, l_)[1]
                        S.dma(q, d4[:, pos + 8 * i_, :, 0:dh], src[:, sl, f0:f0 + H * dh].rearrange("p (h d) -> p h d", h=H),
                              writes=[key], sem=key, nowaw=True)


class ABuilder:
    def __init__(self, ctx, S):
        nc = self.nc = ctx.nc
        self.S = S
        _, self.kg = kside_tensors(ctx, 0)
        self.q1 = ctx.mid("q1_0", [128, NT * 512], BF16)
        self.q2 = ctx.mid("q2_0", [128, NT * 512], BF16)
        self.iw = ctx.mid("iw", [128, NT * 8], F32)
        self.cbias = ctx.ein("cbias", [128, 2 * 512], F32)
        self.identd = ctx.ein("ident", [128, 128], F32)
        self.oa = ctx.mid("oa0", [128, NT * 512], F32)
        self.dbg = None
        self.build()
        S.end_phase()

    def build(self):
        S, nc = self.S, self.nc
        sb = S.sb
        ident_f = sb("ident_f", [128, 128], F32)
        ident = sb("ident", [128, 128], BF16)
        S.dma('sp', ident_f[:], self.identd[:, :], writes=['ident_f'], sem='c0')
        S.op('dve', lambda e: e.tensor_copy(ident[:], ident_f[:]), reads=['ident_f'], writes=['ident'])
        iw = sb("iw", [128, NT * 8], F32)
        S.dma('sp', iw[:], self.iw[:, :], writes=['iw'], sem='c1')
        cbias = sb("cbias", [128, 1024], F32)
        S.dma('sp', cbias[:], self.cbias[:, :], writes=['cbias'], sem='c2')
        akT = sb("akT", [128, 64 * 128], BF16)
        ikT = sb("ikT", [128, 64 * 128], BF16)
        av = sb("av", [128, 64 * 65], BF16)
        load_kside_global(S, 'sp', ikT[:], self.kg, 0, 128, 'ikT', 0, 1, 128, 128)
        load_kside_global(S, 'act', akT[:], self.kg, 0, 0, 'akT', 0, 1, 128, 128)
        load_kside_global(S, ('act', 'pool'), av[:], self.kg, 0, 384, 'av', 0, 1, 65, 65)
        score = [sb("score%d" % i, [128, 8192], F32) for i in range(2)]
        dtmp = [sb("dtmp%d" % i, [128, 512], F32) for i in range(2)]
        iq = [sb("iq%d" % i, [128, 512], BF16) for i in range(2)]
        aq = [sb("aq%d" % i, [128, 512], BF16) for i in range(2)]
        diagw = [sb("diagw%d" % i, [128, 1024], BF16) for i in range(2)]
        r = [sb("r%d" % i, [128, 512], BF16) for i in range(6)]
        mrow = [sb("mrow%d" % i, [128, 8192], BF16) for i in range(2)]
        mT = [sb("mT%d" % i, [128, 512], BF16) for i in range(2)]
        bT = [sb("bT%d" % i, [128, 1024], BF16) for i in range(2)]
        negb = sb("negb", [128, 1], F32)
        S.op('pool', lambda e: e.memset(negb[:], -30000.0), writes=['negb'])
        pT = [[sb("pT%d_%d" % (e, i), [128, 512], BF16) for i in range(3)] for e in range(2)]
        st = {n: [sb("%s%d" % (n, i), [128, 1], F32) for i in range(2)] for n in ('lo', 'hi', 'mid', 'cnt', 'm1', 'tt')}
        htab = [sb("htab%d" % i, [128, NIT + 1], F32) for i in range(2)]
        pw = sb("pw", [128, NIT + 1], F32)
        for it in range(NIT + 1):
            S.op('pool', lambda e: e.memset(pw[:, it:it + 1], 2.0 ** (-it)), writes=['pw'])
        iota8 = sb("iota8", [128, 8], F32)
        for i in range(8):
            S.op('pool', lambda e: e.memset(iota8[:, i:i + 1], float(i)), writes=['iota8'])
        dsc = sb("dsc", [128, 8192], F32)
        junk = dsc[:].bitcast(BF16)
        m8 = sb("m8", [128, 8], F32)
        oh = sb("oh", [128, 8], F32)
        geu = [sb("geu%d" % i, [128, 1], mybir.dt.uint32) for i in range(2)]
        ltu = [sb("ltu%d" % i, [128, 1], mybir.dt.uint32) for i in range(2)]
        rec = sb("rec", [128, 8], F32)
        oast1 = sb("oast0", [128, 512], F32)
        oast = [oast1, oast1]
        xps = [S.ps("xps%d" % i, [128, 512], F32) for i in range(2)]
        scps = S.ps("scps", [128, 512], F32)
        sps = [S.ps("sps%d" % i, [128, 512], F32) for i in range(2)]
        mTps = S.ps("mTps", [128, 1024], BF16)
        O = [S.ps("O%d" % i, [128, 512], F32) for i in range(2)]

        SEQ_A = list(range(0, NT, 2)) + list(range(NT - 1, 0, -2))
        ppar = {k_: i_ % 2 for i_, k_ in enumerate(SEQ_A)}

        def stage1(k):
            par = ppar[k]
            S.dma('sp', iq[par][:], self.q2[:, k * 512:(k + 1) * 512], writes=['iq%d' % par], sem='iq%d' % par)
            S.dma('sp', aq[ppar[k]][:], self.q1[:, k * 512:(k + 1) * 512], writes=['aq%d' % ppar[k]], sem='aq%d' % ppar[k])
            for h in range(8):
                S.op('act', lambda e: e.activation(diagw[par][:, h * 128:(h + 1) * 128], ident_f[:], AF.Copy,
                                                   scale=iw[:, k * 8 + h:k * 8 + h + 1]),
                     reads=['ident_f', 'iw'], writes=['diagw%d' % par])
            ri = [0]

            xb = [(xps[0], 'xps0'), (xps[1], 'xps1'), (sps[0], 'sps0'), (sps[1], 'sps1')]

            def xmm(c, h):
                e_, p_ = h % 2, h // 2
                xt_, xk_ = xb[h % 4]
                S.op('pe', lambda e: e.matmul(xt_[:], iq[par][64 * e_:64 * e_ + 64, p_ * 128:(p_ + 1) * 128],
                                              ikT[64 * e_:64 * e_ + 64, c * 512:(c + 1) * 512], start=True, stop=True),
                     reads=['iq%d' % par, 'ikT'], writes=[xk_])
                S.op('act', lambda e: e.activation(r[h % 6][:], xt_[:], AF.Relu), reads=[xk_], writes=['r%d' % (h % 6)])

            def dmm(c, h):
                S.op('pe', lambda e: e.matmul(scps[:], diagw[par][:, h * 128:(h + 1) * 128], r[h % 6][:],
                                              start=(h == 0), stop=(h == 7)),
                     reads=['diagw%d' % par, 'r%d' % (h % 6)], writes=['scps'])

            for c in range(k + 1):
                for h in range(3):
                    xmm(c, h)
                for h in range(3, 8):
                    xmm(c, h)
                    dmm(c, h - 3)
                for h in range(5, 8):
                    dmm(c, h)
                S.op('act', lambda e: e.activation(score[par][:, c * 512:(c + 1) * 512], scps[:], AF.Copy),
                     reads=['scps'], writes=['score%d' % par])
                yield
            cb = cbias[:, (k % 2) * 512:(k % 2 + 1) * 512]
            dg = score[par][:, k * 512:(k + 1) * 512]
            S.op('pool', lambda e: e.tensor_tensor(dtmp[par][:], dg, cb, ALU.subtract),
                 reads=['score%d' % par, 'cbias'], writes=['dtmp%d' % par])
            S.op('pool', lambda e: e.tensor_tensor(dg, dg, cb, ALU.add),
                 reads=['score%d' % par, 'cbias'], writes=['score%d' % par])

        def stage2(k):
            par = ppar[k]
            N = 512 * (k + 1)
            sc = score[par][:, 0:N]
            lo, hi, mid, cnt, m1 = (st[n][par] for n in ('lo', 'hi', 'mid', 'cnt', 'm1'))
            kl, kh, km, kc, k1 = ('%s%d' % (n, par) for n in ('lo', 'hi', 'mid', 'cnt', 'm1'))
            skey = 'score%d' % par
            ht, kht = htab[par], 'htab%d' % par
            tt, ktt = st['tt'][par], 'tt%d' % par
            S.op('dve', lambda e: e.tensor_reduce(hi[:], sc, AX.X, ALU.max), reads=[skey], writes=[kh])
            S.op('dve', lambda e: e.tensor_reduce(lo[:], dtmp[par][:], AX.X, ALU.min), reads=['dtmp%d' % par], writes=[kl])
            if k > 0:
                S.op('dve', lambda e: e.tensor_reduce(m1[:], score[par][:, 0:N - 512], AX.X, ALU.min), reads=[skey], writes=[k1])
                S.op('dve', lambda e: e.tensor_tensor(lo[:], lo[:], m1[:], ALU.min), reads=[kl, k1], writes=[kl])
            S.op('dve', lambda e: e.tensor_tensor(tt[:], hi[:], lo[:], ALU.subtract), reads=[kh, kl], writes=[ktt])
            S.op('dve', lambda e: e.tensor_scalar(mid[:], lo[:], hi[:, 0:1], 0.5, ALU.add, ALU.mult), reads=[kl, kh], writes=[km])
            S.op('dve', lambda e: e.tensor_scalar(ht[:], pw[:], tt[:, 0:1], 0.501, ALU.mult, ALU.mult), reads=['pw', ktt], writes=[kht])
            for it in range(NIT):
                S.op('dve', lambda e: e.tensor_scalar(junk[:, 0:N], sc, mid[:, 0:1], 0.0, ALU.is_ge, ALU.add, accum_out=cnt[:]),
                     reads=[skey, km], writes=['dsc', kc])
                S.op('dve', lambda e: e.scalar_tensor_tensor(tt[:], cnt[:], 255.5, ht[:, it:it + 1], ALU.is_ge, ALU.mult),
                     reads=[kc, kht], writes=[ktt])
                S.op('dve', lambda e: e.scalar_tensor_tensor(mid[:], tt[:], ht[:, it + 1:it + 2], mid[:], ALU.subtract, ALU.add),
                     reads=[ktt, kht, km], writes=[km])
            S.op('dve', lambda e: e.tensor_tensor(lo[:], mid[:], ht[:, NIT:NIT + 1], ALU.subtract), reads=[km, kht], writes=[kl])
            S.op('dve', lambda e: e.tensor_tensor(hi[:], mid[:], ht[:, NIT:NIT + 1], ALU.add), reads=[km, kht], writes=[kh])
            mk = 'mrow%d' % par
            S.op('dve', lambda e: e.tensor_scalar(mrow[par][:, 0:N], sc, hi[:, 0:1], 0.0, ALU.is_ge, ALU.add, accum_out=cnt[:]),
                 reads=[skey, kh], writes=[mk, kc])
            S.op('dve', lambda e: e.scalar_tensor_tensor(dsc[:, 0:N], mrow[par][:, 0:N], NEG, sc, ALU.mult, ALU.add),
                 reads=[skey, mk], writes=['dsc'])
            S.op('dve', lambda e: e.max(out=m8[:], in_=dsc[:, 0:N]), reads=['dsc'], writes=['m8'])
            S.op('dve', lambda e: e.tensor_scalar(m8[:], m8[:], lo[:, 0:1], None, ALU.subtract), reads=['m8', kl], writes=['m8'])
            S.op('dve', lambda e: e.tensor_scalar(tt[:], cnt[:], -1.0, 255.0, ALU.mult, ALU.add), reads=[kc], writes=[ktt])
            S.op('dve', lambda e: e.tensor_scalar(oh[:], iota8[:], tt[:, 0:1], None, ALU.is_equal), reads=['iota8', ktt], writes=['oh'])
            S.op('dve', lambda e: e.tensor_tensor(oh[:], oh[:], m8[:], ALU.mult), reads=['oh', 'm8'], writes=['oh'])
            S.op('dve', lambda e: e.tensor_reduce(tt[:], oh[:], AX.X, ALU.add), reads=['oh'], writes=[ktt])
            S.op('dve', lambda e: e.tensor_scalar(tt[:], tt[:], 0.0, None, ALU.max), reads=[ktt], writes=[ktt])
            S.op('dve', lambda e: e.tensor_tensor(lo[:], lo[:], tt[:], ALU.add), reads=[kl, ktt], writes=[kl])
            S.op('dve', lambda e: e.tensor_scalar(mrow[par][:, 0:N], sc, lo[:, 0:1], None, ALU.is_ge),
                 reads=[skey, kl], writes=['mrow%d' % par])

        def stage3(k):
            par = ppar[k]
            nkt = 4 * k + 4
            lo = st['lo'][par]
            kl = 'lo%d' % par
            skey = 'score%d' % par

            def smm(j):
                c, u = j // 4, j % 4
                for e_ in range(2):
                    pebias = (u == 3)
                    sp_, sk_ = sb4[(e_, j % 2)]
                    S.op('pe', lambda e: e.matmul(sp_[:], akT[64 * e_:64 * e_ + 64, j * 128:(j + 1) * 128],
                                                  aq[ppar[k]][64 * e_:64 * e_ + 64, :], start=True, stop=not pebias),
                         reads=['akT', 'aq%d' % ppar[k]], writes=[sk_], inc=not pebias)
                    if pebias:
                        S.op('pe', lambda e: e.matmul(sp_[:], ident[:], bT[c % 2][:, 0:512],
                                                      start=False, stop=True),
                             reads=['ident', 'bT%d' % (c % 2)], writes=[sk_])
                    pt = pT[e_][j % 3]
                    pk = 'pT%d_%d' % (e_, j % 3)
                    S.op('act', lambda e: e.activation(pt[:], sp_[:], AF.Exp, scale=0.125),
                         reads=[sk_], writes=[pk])
                    if pebias:
                        continue
                    S.op('pool', lambda e: e.tensor_tensor(
                        pt[:].rearrange("p (s q) -> p s q", s=4), pt[:].rearrange("p (s q) -> p s q", s=4),
                        mT[c % 2][:, u * 128:(u + 1) * 128].unsqueeze(1).to_broadcast([128, 4, 128]), ALU.mult),
                        reads=[pk, 'mT%d' % (c % 2)], writes=[pk])

            def pvmm(j):
                for e_ in range(2):
                    pt = pT[e_][j % 3]
                    pk = 'pT%d_%d' % (e_, j % 3)
                    for p_ in range(4):
                        h = 2 * p_ + e_
                        S.op('pe', lambda e: e.matmul(O[h // 4][:, (h % 4) * 65:(h % 4) * 65 + 65],
                                                      pt[:, p_ * 128:(p_ + 1) * 128], av[:, j * 65:(j + 1) * 65],
                                                      start=(j == 0 and h % 4 == 0), stop=(j == nkt - 1),
                                                      skip_group_check=True),
                             reads=[pk, 'av'], writes=['O%d' % (h // 4)], inc=(p_ == 3))

            if DEBUG and k < 2:
                S.dma('sp', self.dbg[:, k * 8200:k * 8200 + 8192], score[par][:], reads=[skey], sem='dbg', final=True)
                dbgst = sb("dbgst%d" % k, [128, 4], F32)
                for i_, n_ in enumerate(('lo', 'hi', 'mid', 'cnt')):
                    S.op('dve', lambda e: e.tensor_copy(dbgst[:, i_:i_ + 1], st[n_][par][:]), reads=['%s%d' % (n_, par)], writes=['dbgst%d' % k])
                S.dma('sp', self.dbg[:, k * 8200 + 8192:k * 8200 + 8196], dbgst[:], reads=['dbgst%d' % k], sem='dbg2', final=True)
            sb4 = {(0, 0): (sps[0], 'sps0'), (0, 1): (xps[0], 'xps0'), (1, 0): (sps[1], 'sps1'), (1, 1): (xps[1], 'xps1')}

            def prep(c):
                mc = c % 2
                for u in range(4):
                    S.op('pe', lambda e: e.transpose(mTps[:, mc * 512 + u * 128:mc * 512 + (u + 1) * 128],
                                                     mrow[par][:, c * 512 + u * 128:c * 512 + (u + 1) * 128], ident[:]),
                         reads=['mrow%d' % par, 'ident'], writes=['mTps%d' % mc], inc=(u == 3))
                S.op('act', lambda e: e.activation(mT[mc][:], mTps[:, mc * 512:(mc + 1) * 512], AF.Copy),
                     reads=['mTps%d' % mc], writes=['mT%d' % mc])
                for ui, u in enumerate((3,)):
                    S.op('act', lambda e: e.activation(
                        bT[mc][:, ui * 512:(ui + 1) * 512].rearrange("p (r q) -> p r q", r=4),
                        mTps[:, mc * 512 + u * 128:mc * 512 + (u + 1) * 128].unsqueeze(1).to_broadcast([128, 4, 128]),
                        AF.Identity, scale=30000.0, bias=negb[:, 0:1]),
                        reads=['mTps%d' % mc, 'negb'], writes=['bT%d' % mc])

            prep(0)
            for c in range(k + 1):
                for u in range(4):
                    j = 4 * c + u
                    smm(j)
                    if j > 1:
                        pvmm(j - 2)
                    if u == 3 and c + 1 <= k:
                        prep(c + 1)
                yield
            pvmm(nkt - 2)
            pvmm(nkt - 1)

        def stage3n(k):
            par = ppar[k]
            for bnk in range(2):
                Ov = O[bnk][:, 0:260].rearrange("p (h d) -> p h d", h=4)
                S.op('dve', lambda e: e.reciprocal(rec[:, bnk * 4:(bnk + 1) * 4], Ov[:, :, 64]),
                     reads=['O%d' % bnk], writes=['rec'])
                S.op('dve', lambda e: e.tensor_tensor(
                    oast[par][:, bnk * 256:(bnk + 1) * 256].rearrange("p (h d) -> p h d", h=4), Ov[:, :, 0:64],
                    rec[:, bnk * 4:(bnk + 1) * 4].unsqueeze(2).to_broadcast([128, 4, 64]), ALU.mult),
                    reads=['O%d' % bnk, 'rec'], writes=['oast0'])
            S.dma('pool', self.oa[:, k * 512:(k + 1) * 512], oast[par][:], reads=['oast0'], sem='oast0', final=True)

        def run(g):
            for _ in g:
                pass

        seq = SEQ_A
        run(stage1(seq[0]))
        stage2(seq[0])
        for i_, k in enumerate(seq):
            nxt = seq[i_ + 1] if i_ + 1 < NT else None
            if nxt is not None:
                run(stage1(nxt))
            run(stage3(k))
            if nxt is not None:
                stage2(nxt)
            stage3n(k)


def make_cbias(c):
    r = c % 4
    out = np.zeros((128, 2, 4, 128), np.float32)
    q = np.arange(128)[:, None]
    s = np.arange(128)[None, :]
    for par in range(2):
        o = r if par == 0 else 3 - r
        for u in range(4):
            if u == o:
                out[:, par, u, :] = np.where(s <= q, 0.0, NEG)
            elif u > o:
                out[:, par, u, :] = NEG
    return out.reshape(128, 1024)


def gather4(per_core):
    out = []
    for c in range(8):
        b = c // 4
        out.append(np.ascontiguousarray(np.concatenate([per_core[4 * b + rp] for rp in range(4)], axis=0)))
    return out


LAMBDA_INIT = 0.8 - 0.6 * float(np.exp(-0.3 * 1))

WCFG = {
    'B': dict(KF=KF0, kf=(256, 128), vf=(449, 130), npad=1, nm=5, shared=True, nv=65, nO=8, vH=1, vdh=130, vdw=130),
    'C': dict(KF=KF1, kf=(0, 512), vf=(512, 520), npad=16, nm=20, shared=False, nv=65, nO=8, vH=8, vdh=64, vdw=65),
    'D': dict(KF=KF1, kf=(1024, 512), vf=(1536, 516), npad=0, nm=4, shared=False, nv=129, nO=8, vH=4, vdh=128, vdw=129),
}


class WBuilder:
    def __init__(self, ctx, S, kind):
        self.kind = kind
        cfg = self.cfg = WCFG[kind]
        nc = self.nc = ctx.nc
        self.S = S
        L = 0 if kind == 'B' else 1
        self.L = L
        _, self.kg = kside_tensors(ctx, L)
        self.q = ctx.mid({'B': 'q3_0', 'C': 'q1_1', 'D': 'q2_1'}[kind], [128, NT * 512], BF16)
        self.maskd = ctx.ein("mask" + kind, [128, 2 * cfg['nm'] * 128], BF16)
        if kind == 'B':
            self.extra = ctx.ein("extraB", [1, 8], F32)
        if kind == 'D':
            self.extra = ctx.ein("extraD", [1, 4 * 64 + 128], F32)
        self.o = ctx.mid("o" + kind, [128, NT * 512], F32)
        self.build()
        S.end_phase()

    def build(self):
        S, nc, cfg, kind = self.S, self.nc, self.cfg, self.kind
        sb = S.sb
        KF, npad, nm, nv = cfg['KF'], cfg['npad'], cfg['nm'], cfg['nv']
        ntl = 64 + npad
        kw, vw = cfg['kf'][1], cfg['vf'][1]
        kall = sb("kall", [128, ntl * kw], BF16)
        vall = sb("vall", [128, ntl * vw], BF16)
        if npad:
            S.op('pool', lambda e: e.memset(kall[:, 0:npad * kw], 0.0), writes=['kall'])
            S.op('pool', lambda e: e.memset(vall[:, 0:npad * vw], 0.0), writes=['vall'])
        if cfg['vdw'] != cfg['vdh']:
            ones = vall[:, npad * vw:].rearrange("p (t h w) -> p t h w", h=cfg['vH'], w=cfg['vdw'])[:, :, :, cfg['vdh']:cfg['vdw']]
            S.op('pool', lambda e: e.memset(ones, 1.0), writes=['vall'])
        load_kside_global(S, 'sp', kall[:], self.kg, self.L, cfg['kf'][0], 'kall', npad, 1, kw, kw)
        load_kside_global(S, ('act', 'pool'), vall[:], self.kg, self.L, cfg['vf'][0], 'vall', npad, cfg['vH'], cfg['vdh'], cfg['vdw'])
        mask = sb("mask", [128, 2 * nm * 128], BF16)
        S.dma('sp', mask[:], self.maskd[:, :], writes=['mask'], sem='c0')
        qb = [sb("q%d" % i, [128, 512], BF16) for i in range(2)]
        pT = [[sb("pT%d_%d" % (e, i), [128, 512], BF16) for i in range(3)] for e in range(2)]
        rec = sb("rec", [128, 8], F32)
        ost = [sb("ost%d" % i, [128, 512], F32) for i in range(2)]
        sps = [S.ps("sps%d" % i, [128, 512], F32) for i in range(4)]
        nOb = 2 if nv == 65 else 3
        per_bank = 4 if nv == 65 else 3
        O = [S.ps("O%d" % i, [128, 512], F32) for i in range(nOb)]
        if kind == 'B':
            sk = sb("sinks", [128, 8], F32)
            S.dma('sp', sk[:], self.extra[0:1, :].partition_broadcast(128), writes=['sinks'], sem='c1')
            S.op('act', lambda e: e.activation(sk[:], sk[:], AF.Exp), reads=['sinks'], writes=['sinks'])
            den = sb("den", [128, 8], F32)
        if kind == 'D':
            ex = sb("extra", [128, 384], F32)
            S.dma('sp', ex[:], self.extra[0:1, :].partition_broadcast(128), writes=['extra'], sem='c1')
            lt = sb("lamtmp", [128, 128], F32)
            ls = sb("lams", [128, 2], F32)
            neglam = sb("neglam", [128, 1], F32)
            S.op('dve', lambda e: e.tensor_tensor(lt[:, 0:64], ex[:, 0:64], ex[:, 64:128], ALU.mult), reads=['extra'], writes=['lamtmp'])
            S.op('dve', lambda e: e.tensor_tensor(lt[:, 64:128], ex[:, 128:192], ex[:, 192:256], ALU.mult), reads=['extra'], writes=['lamtmp'])
            S.op('dve', lambda e: e.tensor_reduce(ls[:], lt[:].rearrange("p (a d) -> p a d", a=2), AX.X, ALU.add), reads=['lamtmp'], writes=['lams'])
            S.op('act', lambda e: e.activation(ls[:], ls[:], AF.Exp), reads=['lams'], writes=['lams'])
            S.op('dve', lambda e: e.tensor_tensor(neglam[:], ls[:, 1:2], ls[:, 0:1], ALU.subtract), reads=['lams'], writes=['neglam'])
            S.op('dve', lambda e: e.tensor_scalar(neglam[:], neglam[:], -LAMBDA_INIT, None, ALU.add), reads=['neglam'], writes=['neglam'])
            sg = sb("subg", [128, 128], F32)
            S.op('dve', lambda e: e.tensor_scalar(sg[:], ex[:, 256:384], 1.0 - LAMBDA_INIT, None, ALU.mult), reads=['extra'], writes=['subg'])
            t1 = sb("t1", [128, 128], F32)
            t2 = sb("t2", [128, 128], F32)
            sqj = sb("sqj", [128, 128], F32)
            ssd = sb("ssd", [128, 4], F32)

        def oloc(m):
            return m // per_bank, (m % per_bank) * nv

        Oc = [sb("Oc%d" % i, [128, nOb * 512], F32) for i in range(2)]
        pend = []

        def epilogue(k, par):
            okey = 'ost%d' % par
            ock = 'Oc%d' % par
            Os = lambda b_: Oc[par][:, b_ * 512:(b_ + 1) * 512]
            if kind in ('B', 'C'):
                for bnk in range(2):
                    Ov = Os(bnk)[:, 0:260].rearrange("p (h d) -> p h d", h=4)
                    rs = rec[:, bnk * 4:(bnk + 1) * 4]
                    if kind == 'B':
                        S.op('dve', lambda e: e.tensor_tensor(den[:, bnk * 4:(bnk + 1) * 4], Ov[:, :, 64], sk[:, bnk * 4:(bnk + 1) * 4], ALU.add),
                             reads=[ock, 'sinks'], writes=['den'])
                        S.op('dve', lambda e: e.reciprocal(rs, den[:, bnk * 4:(bnk + 1) * 4]), reads=['den'], writes=['rec'])
                    else:
                        S.op('dve', lambda e: e.reciprocal(rs, Ov[:, :, 64]), reads=[ock], writes=['rec'])
                    S.op('dve', lambda e: e.tensor_tensor(
                        ost[par][:, bnk * 256:(bnk + 1) * 256].rearrange("p (h d) -> p h d", h=4), Ov[:, :, 0:64],
                        rs.unsqueeze(2).to_broadcast([128, 4, 64]), ALU.mult),
                        reads=[ock, 'rec'], writes=[okey])
            else:
                for m in range(8):
                    bnk, off = oloc(m)
                    S.op('dve', lambda e: e.reciprocal(rec[:, m:m + 1], Os(bnk)[:, off + 128:off + 129]),
                         reads=[ock], writes=['rec'])
                for h in range(4):
                    b1, o1 = oloc(2 * h)
                    b2, o2 = oloc(2 * h + 1)
                    S.op('dve', lambda e: e.tensor_scalar(t1[:], Os(b1)[:, o1:o1 + 128], rec[:, 2 * h:2 * h + 1], None, ALU.mult),
                         reads=[ock, 'rec'], writes=['t1'])
                    S.op('dve', lambda e: e.tensor_scalar(t2[:], Os(b2)[:, o2:o2 + 128], rec[:, 2 * h + 1:2 * h + 2], None, ALU.mult),
                         reads=[ock, 'rec'], writes=['t2'])
                    S.op('dve', lambda e: e.scalar_tensor_tensor(ost[par][:, h * 128:(h + 1) * 128], t2[:], neglam[:, 0:1], t1[:],
                                                                 ALU.mult, ALU.add),
                         reads=['t1', 't2', 'neglam'], writes=[okey])
                    S.op('act', lambda e: e.activation(sqj[:], ost[par][:, h * 128:(h + 1) * 128], AF.Square, accum_out=ssd[:, h:h + 1]),
                         reads=[okey], writes=['sqj', 'ssd'])
                S.op('dve', lambda e: e.tensor_scalar(ssd[:], ssd[:], 1.0 / 128, EPS, ALU.mult, ALU.add), reads=['ssd'], writes=['ssd'])
                S.op('act', lambda e: e.activation(ssd[:], ssd[:], AF.Ln), reads=['ssd'], writes=['ssd'])
                S.op('act', lambda e: e.activation(ssd[:], ssd[:], AF.Exp, scale=-0.5), reads=['ssd'], writes=['ssd'])
                o3 = ost[par][:].rearrange("p (h d) -> p h d", h=4)
                S.op('dve', lambda e: e.tensor_tensor(o3, o3, ssd[:].unsqueeze(2).to_broadcast([128, 4, 128]), ALU.mult),
                     reads=[okey, 'ssd'], writes=[okey])
                S.op('dve', lambda e: e.tensor_tensor(o3, o3, sg[:].unsqueeze(1).to_broadcast([128, 4, 128]), ALU.mult),
                     reads=[okey, 'subg'], writes=[okey])
            S.dma('sp', self.o[:, k * 512:(k + 1) * 512], ost[par][:], reads=[okey], sem=okey, final=True)

        for k in range(NT):
            par = k % 2
            qk = 'q%d' % par
            S.dma('sp', qb[par][:], self.q[:, k * 512:(k + 1) * 512], writes=[qk], sem=qk)
            if kind == 'D':
                tiles = [(j, (j - 4 * k) if j >= 4 * k else None) for j in range(4 * k + 4)]
            else:
                tiles = [(4 * k + jj, jj) for jj in range(nm)]
            started = set()
            ntile = len(tiles)

            def smm(idx):
                pos, mi = tiles[idx]
                if cfg['shared']:
                    for e_ in range(2):
                        si = 2 * e_ + idx % 2
                        S.op('pe', lambda e: e.matmul(sps[si][:], kall[64 * e_:64 * e_ + 64, pos * kw:pos * kw + 128],
                                                      qb[par][64 * e_:64 * e_ + 64, :], start=True, stop=True),
                             reads=['kall', qk], writes=['sps%d' % si])
                else:
                    for p_ in range(4):
                        for e_ in range(2):
                            si = 2 * e_ + idx % 2
                            S.op('pe', lambda e: e.matmul(sps[si][:, p_ * 128:(p_ + 1) * 128],
                                                          kall[64 * e_:64 * e_ + 64, pos * kw + p_ * 128:pos * kw + (p_ + 1) * 128],
                                                          qb[par][64 * e_:64 * e_ + 64, p_ * 128:(p_ + 1) * 128], start=True, stop=True),
                                 reads=['kall', qk], writes=['sps%d' % si], inc=(p_ == 3))
                for e_ in range(2):
                    si = 2 * e_ + idx % 2
                    pt = pT[e_][idx % 3]
                    pk = 'pT%d_%d' % (e_, idx % 3)
                    S.op('act', lambda e: e.activation(pt[:], sps[si][:], AF.Exp, scale=0.125),
                         reads=['sps%d' % si], writes=[pk])
                    if mi is not None:
                        mo = (par * nm + mi) * 128
                        S.op('dve', lambda e: e.tensor_tensor(
                            pt[:].rearrange("p (s q) -> p s q", s=4), pt[:].rearrange("p (s q) -> p s q", s=4),
                            mask[:, mo:mo + 128].unsqueeze(1).to_broadcast([128, 4, 128]), ALU.mult),
                            reads=[pk, 'mask'], writes=[pk])

            def pvmm(idx):
                pos, mi = tiles[idx]
                for e_ in range(2):
                    pt = pT[e_][idx % 3]
                    pk = 'pT%d_%d' % (e_, idx % 3)
                    for p_ in range(4):
                        if kind == 'B':
                            m = 4 * e_ + p_
                            vo = pos * vw + e_ * 65
                        elif kind == 'C':
                            m = 2 * p_ + e_
                            vo = pos * vw + m * 65
                        else:
                            m = 2 * p_ + e_
                            vo = pos * vw + p_ * 129
                        bnk, off = oloc(m)
                        st_ = bnk not in started
                        started.add(bnk)
                        S.op('pe', lambda e: e.matmul(O[bnk][:, off:off + nv], pt[:, p_ * 128:(p_ + 1) * 128],
                                                      vall[:, vo:vo + nv], start=st_, stop=(idx == ntile - 1),
                                                      skip_group_check=True),
                             reads=[pk, 'vall'], writes=['O%d' % bnk], inc=(p_ == 3))

            for idx in range(ntile):
                smm(idx)
                if idx > 1:
                    pvmm(idx - 2)
                if idx == min(3, ntile - 1) and pend:
                    epilogue(*pend.pop())
            pvmm(ntile - 2)
            pvmm(ntile - 1)
            used = per_bank * nv
            for bnk in range(nOb):
                S.op('dve', lambda e: e.tensor_copy(Oc[par][:, bnk * 512:bnk * 512 + used], O[bnk][:, 0:used]),
                     reads=['O%d' % bnk], writes=['Oc%d' % par])
            pend.append((k, par))
        epilogue(*pend.pop())


def make_wmask(c, kind):
    r = c % 4
    nm = WCFG[kind]['nm']
    out = np.zeros((128, 2, nm, 128), np.float32)
    s = np.arange(128)[:, None]
    q = np.arange(128)[None, :]
    for par in range(2):
        o = r if par == 0 else 3 - r
        for jj in range(nm):
            if kind == 'B':
                dt = o + 1 - jj
            elif kind == 'C':
                dt = o + 16 - jj
            else:
                dt = o - jj
            dl = 128 * dt + q - s
            if kind == 'B':
                m = ((dl >= 0) & (dl < 128)).astype(np.float32)
            elif kind == 'D':
                m = (dl >= 0).astype(np.float32)
            else:
                m = ((dl >= 0) & (dl <= 128)).astype(np.float32) \
                    + ((dl >= 0) & (dl <= 512) & (dl % 4 == 0)).astype(np.float32) \
                    + ((dl >= 0) & (dl <= 2048) & (dl % 16 == 0)).astype(np.float32)
            out[:, par, jj, :] = m
    import ml_dtypes
    return out.reshape(128, 2 * nm * 128).astype(ml_dtypes.bfloat16)


class PostBuilder:
    def __init__(self, ctx, S, layer):
        nc = self.nc = ctx.nc
        self.S = S
        L = layer
        self.oa = ctx.mid("oa0" if L == 0 else "oC", [128, NT * 512], F32)
        self.ob = ctx.mid("oB" if L == 0 else "oD", [128, NT * 512], F32)
        self.gates = ctx.mid("gates%d" % L, [128, NT * 1024], F32)
        self.x = ctx.ein("x", [128, NT * 1024], F32) if L == 0 else ctx.mid("x1", [128, NT * 1024], F32)
        self.p = ctx.ein("p%d" % L, [128, NT * 256], F32)
        self.w_out = ctx.ein("w_out%d" % L, [1024, 1024], F32)
        self.w_gate = ctx.ein("w_gate%d" % L, [1024, 1024], F32)
        self.w_proj = ctx.ein("w_proj%d" % L, [256, 1024], F32)
        self.pg = ctx.ein("pg%d" % L, [128, 8], F32)
        self.identd = ctx.ein("ident", [128, 128], F32)
        self.xo = ctx.mid("x1", [128, NT * 1024], F32) if L == 0 else ctx.eout("y", [128, NT * 1024], F32)
        self.build()
        S.end_phase()

    def build(self):
        S, nc = self.S, self.nc
        sb = S.sb
        ident_f = sb("ident_f", [128, 128], F32)
        ident = sb("ident", [128, 128], BF16)
        S.dma('sp', ident_f[:], self.identd[:, :], writes=['ident_f'], sem='c0')
        S.op('dve', lambda e: e.tensor_copy(ident[:], ident_f[:]), reads=['ident_f'], writes=['ident'])
        pg = sb("pg", [128, 8], F32)
        S.dma('sp', pg[:], self.pg[:, :], writes=['pg'], sem='c1')
        wo = sb("wo", [128, 8 * 1024], BF16)
        wg = sb("wg", [128, 8 * 1024], BF16)
        wp = sb("wp", [128, 2 * 1024], BF16)
        wst = [sb("wst%d" % i, [128, 1024], F32) for i in range(2)]
        wi = 0
        for wi_m, (src, dst, nch, scale) in enumerate(((self.w_out, wo, 8, False), (self.w_gate, wg, 8, True), (self.w_proj, wp, 2, False))):
            for c in range(nch):
                b = wi % 2
                wi += 1
                S.dma('sp', wst[b][:], src[c * 128:(c + 1) * 128, :], writes=['wst%d' % b], sem='wst%d' % b)
                wkey = 'w%d_%d' % (wi_m, c)
                if scale:
                    S.op('act', lambda e: e.activation(dst[:, c * 1024:(c + 1) * 1024], wst[b][:], AF.Copy, scale=pg[:, c:c + 1]),
                         reads=['wst%d' % b, 'pg'], writes=[wkey])
                else:
                    S.op('dve', lambda e: e.tensor_copy(dst[:, c * 1024:(c + 1) * 1024], wst[b][:]),
                         reads=['wst%d' % b], writes=[wkey])
        ot = [sb("ot%d" % i, [128, 1024], F32) for i in range(2)]
        gt = [sb("gt%d" % i, [128, 1024], F32) for i in range(2)]
        xt = [sb("xt%d" % i, [128, 1024], F32) for i in range(3)]
        ptl = [sb("ptl%d" % i, [128, 256], F32) for i in range(2)]
        y = [sb("y%d" % i, [128, 1024], BF16) for i in range(2)]
        yT = [sb("yT%d" % i, [128, 1024], BF16) for i in range(2)]
        x1 = [sb("x1_%d" % i, [128, 1024], F32) for i in range(3)]
        sqj = sb("sqj", [128, 1024], F32)
        ss = sb("ss", [128, 1], F32)
        hb = sb("hb", [128, 1024], BF16)
        hT = [sb("hT%d" % i, [128, 1024], BF16) for i in range(3)]
        sig = sb("sig", [128, 1024], F32)
        pb = sb("pb", [128, 256], BF16)
        ppT = [sb("ppT%d" % i, [128, 256], BF16) for i in range(4)]
        tmp = sb("tmp", [128, 1024], F32)
        xo = [sb("xo%d" % i, [128, 1024], F32) for i in range(2)]
        tp = [S.ps("tp%d" % i, [128, 1024], BF16) for i in range(2)]
        mo = [S.ps("mo%d" % i, [128, 512], F32) for i in range(2)]
        mg = [S.ps("mg%d" % i, [128, 512], F32) for i in range(2)]
        mp = [S.ps("mp%d" % i, [128, 512], F32) for i in range(2)]

        def loads(k):
            b = k % 2
            S.dma('sp', ot[b][:, 0:512], self.oa[:, k * 512:(k + 1) * 512], writes=['ot%d' % b], sem='ota%d' % b)
            S.dma('sp', ot[b][:, 512:1024], self.ob[:, k * 512:(k + 1) * 512], writes=['ot%d' % b], sem='otb%d' % b, nowaw=True)
            S.dma('sp', gt[b][:], self.gates[:, k * 1024:(k + 1) * 1024], writes=['gt%d' % b], sem='gt%d' % b)
            S.dma('sp', xt[k % 3][:], self.x[:, k * 1024:(k + 1) * 1024], writes=['xt%d' % (k % 3)], sem='xt%d' % (k % 3))
            S.dma('sp', ptl[b][:], self.p[:, k * 256:(k + 1) * 256], writes=['ptl%d' % b], sem='ptl%d' % b)

        def stA(k):
            b = k % 2
            S.op('dve', lambda e: e.tensor_tensor(y[b][:], ot[b][:], gt[b][:], ALU.mult), reads=['ot%d' % b, 'gt%d' % b], writes=['y%d' % b])
            for c in range(8):
                S.op('pe', lambda e: e.transpose(tp[0][:, c * 128:(c + 1) * 128], y[b][:, c * 128:(c + 1) * 128], ident[:]),
                     reads=['y%d' % b, 'ident'], writes=['tp0'], inc=(c == 7))
            S.op('act', lambda e: e.activation(yT[b][:], tp[0][:], AF.Copy), reads=['tp0'], writes=['yT%d' % b])
            S.op('pool', lambda e: e.tensor_copy(pb[:], ptl[b][:]), reads=['ptl%d' % b], writes=['pb'])
            for c in range(2):
                S.op('pe', lambda e: e.transpose(tp[0][:, c * 128:(c + 1) * 128], pb[:, c * 128:(c + 1) * 128], ident[:]),
                     reads=['pb', 'ident'], writes=['tp0'], inc=(c == 1))
            S.op('act', lambda e: e.activation(ppT[k % 4][:], tp[0][:, 0:256], AF.Copy), reads=['tp0'], writes=['ppT%d' % (k % 4)])

        def stB(k):
            b = k % 2
            for h in range(2):
                for c in range(8):
                    S.op('pe', lambda e: e.matmul(mo[h][:], yT[b][:, c * 128:(c + 1) * 128], wo[:, c * 1024 + h * 512:c * 1024 + (h + 1) * 512],
                                                  start=(c == 0), stop=(c == 7)), reads=['yT%d' % b, 'w0_%d' % c], writes=['mo%d' % h], inc=(c == 7))
                S.op('dve', lambda e: e.tensor_tensor(x1[k % 3][:, h * 512:(h + 1) * 512], mo[h][:], xt[k % 3][:, h * 512:(h + 1) * 512], ALU.add),
                     reads=['mo%d' % h, 'xt%d' % (k % 3)], writes=['x1_%d' % (k % 3)])
            S.op('act', lambda e: e.activation(sqj[:], x1[k % 3][:], AF.Square, accum_out=ss[:]), reads=['x1_%d' % (k % 3)], writes=['sqj', 'ss'])
            S.op('dve', lambda e: e.tensor_scalar(ss[:], ss[:], 1.0 / 1024, EPS, ALU.mult, ALU.add), reads=['ss'], writes=['ss'])
            S.op('act', lambda e: e.activation(ss[:], ss[:], AF.Ln), reads=['ss'], writes=['ss'])
            S.op('act', lambda e: e.activation(ss[:], ss[:], AF.Exp, scale=-0.5), reads=['ss'], writes=['ss'])
            S.op('act', lambda e: e.activation(hb[:], x1[k % 3][:], AF.Copy, scale=ss[:]), reads=['x1_%d' % (k % 3), 'ss'], writes=['hb'])

        def stB2(k):
            b = k % 2
            for c in range(8):
                S.op('pe', lambda e: e.transpose(tp[1][:, c * 128:(c + 1) * 128], hb[:, c * 128:(c + 1) * 128], ident[:]),
                     reads=['hb', 'ident'], writes=['tp1'], inc=(c == 7))
            S.op('dve', lambda e: e.tensor_copy(hT[k % 3][:], tp[1][:]), reads=['tp1'], writes=['hT%d' % (k % 3)])

        def stC(k):
            b = k % 2
            for h in range(2):
                for c in range(8):
                    S.op('pe', lambda e: e.matmul(mg[h][:], hT[k % 3][:, c * 128:(c + 1) * 128], wg[:, c * 1024 + h * 512:c * 1024 + (h + 1) * 512],
                                                  start=(c == 0), stop=(c == 7)), reads=['hT%d' % (k % 3), 'w1_%d' % c], writes=['mg%d' % h], inc=(c == 7))
                S.op('act', lambda e: e.activation(sig[:, h * 512:(h + 1) * 512], mg[h][:], AF.Sigmoid), reads=['mg%d' % h], writes=['sig'])
            for h in range(2):
                for c in range(2):
                    S.op('pe', lambda e: e.matmul(mp[h][:], ppT[k % 4][:, c * 128:(c + 1) * 128], wp[:, c * 1024 + h * 512:c * 1024 + (h + 1) * 512],
                                                  start=(c == 0), stop=(c == 1)), reads=['ppT%d' % (k % 4), 'w2_%d' % c], writes=['mp%d' % h], inc=(c == 1))
                S.op('dve', lambda e: e.tensor_tensor(tmp[:, h * 512:(h + 1) * 512], mp[h][:], sig[:, h * 512:(h + 1) * 512], ALU.mult),
                     reads=['mp%d' % h, 'sig'], writes=['tmp'])
            S.op('pool', lambda e: e.tensor_tensor(xo[b][:], x1[k % 3][:], tmp[:], ALU.add), reads=['x1_%d' % (k % 3), 'tmp'], writes=['xo%d' % b])
            S.dma('sp', self.xo[:, k * 1024:(k + 1) * 1024], xo[b][:], reads=['xo%d' % b], sem='xo%d' % b, final=True)

        loads(0)
        for t in range(NT + 3):
            if t + 1 < NT:
                loads(t + 1)
            if 0 <= t - 3 < NT:
                stC(t - 3)
            if 0 <= t - 1 < NT:
                stB(t - 1)
            if t < NT:
                stA(t)
            if 0 <= t - 1 < NT:
                stB2(t - 1)


GROUPS = [[0, 1, 2, 3], [4, 5, 6, 7]]
_prog = {}


def gather_phase(ctx, S, L):
    ks, kg = kside_tensors(ctx, L)
    for a, b in zip(ks, kg):
        S.allgather(a[:, :], b[:, :], GROUPS)
    S.end_phase()


def build_fused():
    nc = bass.Bass("TRN2", target_bir_lowering=False)
    S = Sched(nc)
    ctx = Ctx(nc)
    PBuilder(ctx, S, 0)
    ABuilder(ctx, S)
    WBuilder(ctx, S, 'B')
    PostBuilder(ctx, S, 0)
    PBuilder(ctx, S, 1)
    WBuilder(ctx, S, 'C')
    WBuilder(ctx, S, 'D')
    PostBuilder(ctx, S, 1)
    S.finish()
    return nc


def kernel(**inp):
    g = lambda k: np.asarray(inp[k])
    x, p = g('x'), g('p')
    if 'nc' not in _prog:
        _prog['nc'] = build_fused()
    nc = _prog['nc']
    ident = np.eye(128, dtype=np.float32)

    def gains(lst):
        o = np.zeros((1, 5 * 64), np.float32)
        for i, v in enumerate(lst):
            o[0, i * 64:(i + 1) * 64] = v
        return o

    shared = {
        "w_in0": np.ascontiguousarray(g('w_in_even')[0]), "w_in1": np.ascontiguousarray(g('w_in_odd')[0]),
        "ng0": kchunk(g('norm_gain')[0]), "ng1": kchunk(g('norm_gain')[1]),
        "gains0": gains([g('a_q_gain')[0], g('a_k_gain')[0], g('idx_k_gain')[0], g('b_q_gain')[0], g('b_k_gain')[0]]),
        "gains1": gains([g('c_q_gain')[0], g('c_k_gain')[0], g('d_q_gain')[0], g('d_k_gain')[0]]),
        "ident": ident,
        "extraB": np.ascontiguousarray(g('b_sinks').reshape(1, 8)).astype(np.float32),
        "extraD": np.concatenate([g('d_lambda_q1')[0], g('d_lambda_k1')[0], g('d_lambda_q2')[0], g('d_lambda_k2')[0],
                                  g('d_subln_gain')[0]]).astype(np.float32).reshape(1, 384),
        "w_out0": np.ascontiguousarray(g('w_out_even')[0]), "w_out1": np.ascontiguousarray(g('w_out_odd')[0]),
        "w_gate0": np.ascontiguousarray(g('w_ple_gate')[0]), "w_gate1": np.ascontiguousarray(g('w_ple_gate')[1]),
        "w_proj0": np.ascontiguousarray(g('w_ple_proj')[0]), "w_proj1": np.ascontiguousarray(g('w_ple_proj')[1]),
        "pg0": kchunk(g('ple_norm_gain')[0]), "pg1": kchunk(g('ple_norm_gain')[1]),
    }
    in_maps = []
    rows = [own_rows(c).reshape(-1) for c in range(8)]
    for c in range(8):
        m = dict(shared)
        m["x"] = tile_major(x[c // 4][rows[c]])
        m["p0"] = tile_major(p[0][c // 4][rows[c]])
        m["p1"] = tile_major(p[1][c // 4][rows[c]])
        m["cs"] = rope_tables(c)
        m["cbias"] = make_cbias(c)
        for kind in 'BCD':
            m["mask" + kind] = make_wmask(c, kind)
        in_maps.append(m)
    res = run_bass_kernel_spmd(nc, in_maps, core_ids=list(range(8)))
    out = np.zeros((2, 8192, 1024), np.float32)
    for c in range(8):
        out[c // 4][rows[c]] = from_tile_major(np.asarray(res.results[c]['y']), 1024)
    return out
```
